# Optimizing a Trainium2 kernel written in Bass

```python
import math
import jax, jax.numpy as jnp
from jax import lax
import numpy as np

D_MODEL = 2048
BATCH = 8
SEQ = 2048
DEPTH = 2

CHUNK = 64
N_META = 16
LEAD = 128
Q_BLOCK = 128
EPS = 1e-6

CONV_DIM = 512
CONV_W = 3
ML_HEADS = 4
ML_DK = 256
ML_DV = 256
ML_DIM = ML_HEADS * ML_DV
DA_HEADS = 4
DA_HD = 64
DA_VD = 2 * DA_HD
DA_DIM = DA_HEADS * DA_VD
D_FF = 5632
FFN_CONV_W = 3
N_BRANCH = 3

D_IN = 3 * CONV_DIM + 2 * ML_HEADS * ML_DK + 2 * ML_DIM + 2 * ML_HEADS + 4 * DA_HEADS * DA_HD + DA_DIM + N_BRANCH * D_MODEL

kernel_name = "hybrid_gated_conv_mlstm_diffattn_encoder"


def _in_split_sizes():
    return (CONV_DIM, CONV_DIM, CONV_DIM,
            ML_HEADS * ML_DK, ML_HEADS * ML_DK, ML_DIM, ML_DIM, ML_HEADS, ML_HEADS,
            DA_HEADS * 2 * DA_HD, DA_HEADS * 2 * DA_HD, DA_DIM,
            D_MODEL, D_MODEL, D_MODEL)


def rmsnorm(x, g):
    xf = x.astype(jnp.float32)
    y = xf * lax.rsqrt(jnp.mean(xf * xf, axis=-1, keepdims=True) + EPS)
    return (y * g.astype(jnp.float32)).astype(x.dtype)


def causal_dwconv(x, w):
    W = w.shape[0]
    L = x.shape[1]
    xp = jnp.pad(x, ((0, 0), (W - 1, 0), (0, 0)))
    y = xp[:, 0:L] * w[0]
    for j in range(1, W):
        y = y + xp[:, j:j + L] * w[j]
    return y


def _pad_lead(t, n):
    return jnp.pad(t, ((0, 0), (n, 0)) + ((0, 0),) * (t.ndim - 2))


def _chunk_ids(n):
    pos = jnp.arange(n)
    return jnp.where(pos < LEAD, 0, 1 + (pos - LEAD) // CHUNK)


def mlstm(q, k, v, i_pre, f_pre):
    padn = LEAD - N_META
    q, k, v, i_pre, f_pre = [_pad_lead(t, padn) for t in (q, k, v, i_pre, f_pre)]
    Bsz, Lp, H, dk = q.shape
    dv = v.shape[-1]
    nc = Lp // CHUNK
    valid = (jnp.arange(Lp) >= padn)[None, :, None]
    log_i = jnp.where(valid, i_pre, -jnp.inf)
    log_f = jnp.where(valid, jax.nn.log_sigmoid(f_pre), 0.0)
    k = k * (dk ** -0.5)

    def to_chunks(t):
        return t.reshape(Bsz, nc, CHUNK, H, t.shape[-1]).transpose(1, 0, 3, 2, 4)

    def gate_chunks(t):
        return t.reshape(Bsz, nc, CHUNK, H).transpose(1, 0, 3, 2)

    tri = jnp.tril(jnp.ones((CHUNK, CHUNK), dtype=bool))

    def step(carry, inp):
        C, n, m = carry
        qc, kc, vc, lic, lfc = inp
        b = jnp.cumsum(lfc, axis=-1)
        D = b[..., :, None] - b[..., None, :] + lic[..., None, :]
        D = jnp.where(tri, D, -jnp.inf)
        inter = b + m[..., None]
        m_t = jnp.maximum(jnp.max(D, axis=-1), inter)
        w = jnp.exp(D - m_t[..., None])
        g = jnp.exp(inter - m_t)
        s = jnp.einsum('bhtd,bhsd->bhts', qc, kc) * w
        num = g[..., None] * jnp.einsum('bhed,bhtd->bhte', C, qc) + jnp.einsum('bhts,bhse->bhte', s, vc)
        den = g * jnp.einsum('bhd,bhtd->bht', n, qc) + jnp.sum(s, axis=-1)
        h = num / jnp.maximum(jnp.abs(den), jnp.exp(-m_t))[..., None]
        bL = b[..., -1]
        ds = bL[..., None] - b + lic
        m_new = jnp.maximum(bL + m, jnp.max(ds, axis=-1))
        wk = jnp.exp(ds - m_new[..., None])
        gs = jnp.exp(bL + m - m_new)
        C = gs[..., None, None] * C + jnp.einsum('bhse,bhsd->bhed', vc * wk[..., None], kc)
        n = gs[..., None] * n + jnp.einsum('bhs,bhsd->bhd', wk, kc)
        return (C, n, m_new), h

    init = (jnp.zeros((Bsz, H, dv, dk), jnp.float32),
            jnp.zeros((Bsz, H, dk), jnp.float32),
            jnp.zeros((Bsz, H), jnp.float32))
    _, hs = lax.scan(step, init, (to_chunks(q), to_chunks(k), to_chunks(v),
                                  gate_chunks(log_i), gate_chunks(log_f)))
    hs = hs.transpose(1, 0, 3, 2, 4).reshape(Bsz, Lp, H, dv)
    return hs[:, padn:]


def diff_attention(q, k, v, lam):
    padn = LEAD - N_META
    q, k, v = [_pad_lead(t, padn) for t in (q, k, v)]
    Lp = q.shape[1]
    cid = _chunk_ids(Lp)
    valid = jnp.arange(Lp) >= padn
    scale = DA_HD ** -0.5
    outs = []
    for j in range(Lp // Q_BLOCK):
        q0 = j * Q_BLOCK
        kend = q0 + Q_BLOCK
        qb = q[:, q0:kend]
        kb = k[:, :kend]
        vb = v[:, :kend]
        s = jnp.einsum('bqhmd,bkhmd->bhmqk', qb, kb).astype(jnp.float32) * scale
        mask = (cid[None, :kend] <= cid[q0:kend, None]) & valid[None, :kend]
        p = jax.nn.softmax(jnp.where(mask, s, -jnp.inf), axis=-1)
        a = p[:, :, 0] - lam * p[:, :, 1]
        outs.append(jnp.einsum('bhqk,bkhe->bqhe', a.astype(v.dtype), vb))
    return jnp.concatenate(outs, axis=1)[:, padn:]


def mixer_block(u, layer, w_in, conv_a, b_if, ml_norm, da_lambda, da_norm, w_br_a, w_br_m, w_br_d, w_out):
    Bsz, L, _ = u.shape
    f32 = jnp.float32
    z = u @ w_in
    idx = np.cumsum(_in_split_sizes())[:-1].tolist()
    (a_x, a_b, a_c, m_q, m_k, m_v, m_o, m_i, m_f,
     d_q, d_k, d_v, g_a, g_m, g_d) = jnp.split(z, idx, axis=-1)
    y_a = (a_b * causal_dwconv(a_c * a_x, conv_a)) @ w_br_a
    hm = mlstm(m_q.reshape(Bsz, L, ML_HEADS, ML_DK).astype(f32),
               m_k.reshape(Bsz, L, ML_HEADS, ML_DK).astype(f32),
               m_v.reshape(Bsz, L, ML_HEADS, ML_DV).astype(f32),
               (m_i + b_if[0]).astype(f32), (m_f + b_if[1]).astype(f32))
    hm = rmsnorm(hm, ml_norm.reshape(ML_HEADS, ML_DV)) * jax.nn.sigmoid(m_o.reshape(Bsz, L, ML_HEADS, ML_DV).astype(f32))
    y_m = hm.reshape(Bsz, L, ML_DIM).astype(u.dtype) @ w_br_m
    lam_init = 0.8 - 0.6 * math.exp(-0.3 * layer)
    lf = da_lambda.astype(f32)
    lam = jnp.exp(jnp.sum(lf[0] * lf[1])) - jnp.exp(jnp.sum(lf[2] * lf[3])) + lam_init
    hd = diff_attention(d_q.reshape(Bsz, L, DA_HEADS, 2, DA_HD),
                        d_k.reshape(Bsz, L, DA_HEADS, 2, DA_HD),
                        d_v.reshape(Bsz, L, DA_HEADS, DA_VD), lam)
    hd = rmsnorm(hd, da_norm) * (1.0 - lam_init)
    y_d = hd.reshape(Bsz, L, DA_DIM) @ w_br_d
    merged = jax.nn.sigmoid(g_a) * y_a + jax.nn.sigmoid(g_m) * y_m + jax.nn.sigmoid(g_d) * y_d
    return merged @ w_out


def channel_mixer(u, w_up, conv_w, conv_b, w_down):
    a, b = jnp.split(u @ w_up, 2, axis=-1)
    a = causal_dwconv(a, conv_w) + conv_b
    return (jax.nn.gelu(a, approximate=False) * b) @ w_down


def setup_inputs(seed: int = 0) -> dict:
    key = jax.random.key(seed)
    ks = jax.random.split(key, 24)
    nrm = lambda k, shape, s: jax.random.normal(k, shape, jnp.float32) * s
    b_if = jnp.stack([nrm(ks[5], (DEPTH, ML_HEADS), 0.1),
                      jnp.linspace(3.0, 6.0, ML_HEADS)[None, :] + nrm(ks[6], (DEPTH, ML_HEADS), 0.1)], axis=1)
    return {
        "x": nrm(ks[0], (BATCH, SEQ, D_MODEL), 1.0),
        "meta": nrm(ks[1], (N_META, D_MODEL), 1.0),
        "norm_mix": 1.0 + nrm(ks[2], (DEPTH, D_MODEL), 0.02),
        "w_in": nrm(ks[3], (DEPTH, D_MODEL, D_IN), D_MODEL ** -0.5),
        "conv_a": nrm(ks[4], (DEPTH, CONV_W, CONV_DIM), CONV_W ** -0.5),
        "b_if": b_if,
        "ml_norm": 1.0 + nrm(ks[7], (DEPTH, ML_DIM), 0.02),
        "da_lambda": nrm(ks[8], (DEPTH, 4, DA_HD), 0.1),
        "da_norm": 1.0 + nrm(ks[9], (DEPTH, DA_VD), 0.02),
        "w_br_a": nrm(ks[10], (DEPTH, CONV_DIM, D_MODEL), CONV_DIM ** -0.5),
        "w_br_m": nrm(ks[11], (DEPTH, ML_DIM, D_MODEL), ML_DIM ** -0.5),
        "w_br_d": nrm(ks[12], (DEPTH, DA_DIM, D_MODEL), DA_DIM ** -0.5),
        "w_out": nrm(ks[13], (DEPTH, D_MODEL, D_MODEL), D_MODEL ** -0.5),
        "norm_ffn": 1.0 + nrm(ks[14], (DEPTH, D_MODEL), 0.02),
        "w_up": nrm(ks[15], (DEPTH, D_MODEL, 2 * D_FF), D_MODEL ** -0.5),
        "conv_ffn": nrm(ks[16], (DEPTH, FFN_CONV_W, D_FF), FFN_CONV_W ** -0.5),
        "conv_ffn_b": nrm(ks[17], (DEPTH, D_FF), 0.02),
        "w_down": nrm(ks[18], (DEPTH, D_FF, D_MODEL), D_FF ** -0.5),
        "norm_f": 1.0 + nrm(ks[19], (D_MODEL,), 0.02),
    }


def reference(x, meta, norm_mix, w_in, conv_a, b_if, ml_norm, da_lambda, da_norm, w_br_a, w_br_m, w_br_d, w_out, norm_ffn, w_up, conv_ffn, conv_ffn_b, w_down, norm_f):
    Bsz = x.shape[0]
    h = jnp.concatenate([jnp.broadcast_to(meta[None].astype(x.dtype), (Bsz, N_META, D_MODEL)), x], axis=1)
    for i in range(DEPTH):
        h = h + mixer_block(rmsnorm(h, norm_mix[i]), i, w_in[i], conv_a[i], b_if[i], ml_norm[i],
                            da_lambda[i], da_norm[i], w_br_a[i], w_br_m[i], w_br_d[i], w_out[i])
        h = h + channel_mixer(rmsnorm(h, norm_ffn[i]), w_up[i], conv_ffn[i], conv_ffn_b[i], w_down[i])
    return rmsnorm(h, norm_f)[:, N_META:]
```

```python
import math, os
ZP = os.environ.get('ZP', 'a,fm,tm,if').split(',')
import numpy as np
import concourse.bass as bass
import concourse.mybir as mybir
from concourse.bass_utils import run_bass_kernel_spmd

F32 = mybir.dt.float32
BF16 = mybir.dt.bfloat16
AF = mybir.ActivationFunctionType
ALU = mybir.AluOpType
AX = mybir.AxisListType

NT = 2176
NTI = 17
D = 2048
KC = 16
D_IN = 13320
DFF = 5632
EPS = 1e-6
TGS = [(0, 512), (512, 512), (1024, 512), (1536, 512), (2048, 128)]
NEG = -30000.0

C_ID, C_TRI, C_MV, C_MD, C_CA, C_CF, C_FB, NCST = 0, 128, 256, 384, 512, 536, 800, 888


class Tok:
    __slots__ = ("w", "r", "x")

    def __init__(self, x=False):
        self.w = None
        self.r = {}
        self.x = x


class Prog:
    def __init__(self, nc, n_dma_sems=8):
        self.nc = nc
        self.eng = {"pe": nc.tensor, "act": nc.scalar, "dve": nc.vector,
                    "pool": nc.gpsimd, "sp": nc.sync}
        self.sems = {}
        self.cnt = {}
        for k in ("pe", "act", "dve", "pool"):
            self.sems[k] = nc.alloc_semaphore("s_" + k)
            self.cnt[k] = 0
        self.seen = {k: {} for k in self.eng}
        self.dring = {}
        for q in ("sp", "pool"):
            self.dring[q] = dict(
                sems=[nc.alloc_semaphore("d_%s%d" % (q, i)) for i in range(n_dma_sems)],
                cnt=[0] * n_dma_sems, nxt=0)
        self.ninstr = 0
        self.log = {k: [] for k in self.eng}

    def tok(self):
        return Tok()

    def toks(self, n):
        return [Tok() for _ in range(n)]

    def _semobj(self, key):
        if isinstance(key, str):
            return self.sems[key]
        q, i = key
        return self.dring[q]["sems"][i]

    def _collect(self, reads, writes):
        need = {}
        for t in reads:
            if t.w is not None:
                k, v = t.w
                if need.get(k, 0) < v:
                    need[k] = v
        for t in writes:
            if t.w is not None:
                k, v = t.w
                if need.get(k, 0) < v:
                    need[k] = v
            for k, v in t.r.items():
                if need.get(k, 0) < v:
                    need[k] = v
        return need

    def _emit_waits(self, e, need, attach=False):
        seen = self.seen[e]
        eng = self.eng[e]
        todo = []
        for k, v in need.items():
            if e == "pe" and k == "pe":
                continue
            if seen.get(k, 0) < v:
                todo.append((k, v))
                seen[k] = v
        held = todo.pop() if (attach and todo) else None
        for k, v in todo:
            eng.wait_ge(self._semobj(k), v)
            self.log[e].append(('w', k, v))
            self.ninstr += 1
        if held is not None:
            self.log[e].append(('w', held[0], held[1]))
        return held

    def _record(self, ev, reads, writes):
        k, v = ev
        for t in reads:
            if t.r.get(k, 0) < v:
                t.r[k] = v
        for t in writes:
            t.w = ev
            t.r = {}

    def op(self, e, fn, reads=(), writes=(), inc=True):
        xr = [t for t in reads if t.x]
        if xr:
            writes = list(writes) + xr
        need = self._collect(reads, writes)
        held = self._emit_waits(e, need, attach=True)
        ins = fn(self.eng[e])
        if held is not None:
            ins._wait_ge(self._semobj(held[0]), held[1])
        self.ninstr += 1
        if inc:
            self.cnt[e] += 1
            ins.then_inc(self.sems[e], 1)
            self.log[e].append(('i', e, 1))
            ev = (e, self.cnt[e])
        else:
            ev = (e, self.cnt[e] + 1)
        self._record(ev, reads, writes)
        return ins

    def dma(self, q, out, in_, reads=(), writes=()):
        ring = self.dring[q]
        i = ring["nxt"]
        ring["nxt"] = (i + 1) % len(ring["sems"])
        key = (q, i)
        need = self._collect(reads, writes)
        if ring["cnt"][i] > 0:
            need[key] = max(need.get(key, 0), 16 * ring["cnt"][i])
        held = self._emit_waits(q, need, attach=True)
        ins = self.eng[q].dma_start(out=out, in_=in_)
        if held is not None:
            ins._wait_ge(self._semobj(held[0]), held[1])
        self.ninstr += 1
        ring["cnt"][i] += 1
        ev = (key, 16 * ring["cnt"][i])
        ins.then_inc(ring["sems"][i], 16)
        self.log[q].append(('i', key, 16))
        self._record(ev, reads, writes)
        return ins

    def _all_events(self):
        need = {}
        for k in ("pe", "act", "dve", "pool"):
            if self.cnt[k] > 0:
                need[k] = self.cnt[k]
        for q, ring in self.dring.items():
            for i, c in enumerate(ring["cnt"]):
                if c > 0:
                    need[(q, i)] = 16 * c
        return need

    def barrier(self):
        need = self._all_events()
        for e in ("sp", "pool", "act", "dve", "pe"):
            n2 = {k: v for k, v in need.items() if k != e}
            self._emit_waits(e, n2)

    def finish(self):
        self._emit_waits("sp", self._all_events())


class Stage:
    def __init__(self, arena, nbytes, p):
        self.a = arena
        self.n = nbytes
        self.off = 0
        self.p = p

    def alloc(self, shape, dt):
        esz = 4 if dt == F32 else 2
        n = 1
        for s in shape[1:]:
            n *= s
        nb = (n * esz + 63) // 64 * 64
        assert self.off + nb <= self.n, ("stage arena overflow", self.off, nb, self.n)
        ap = self.a[:, self.off // 2: self.off // 2 + n * esz // 2]
        self.off += nb
        if dt == F32:
            ap = ap.bitcast(F32)
        if len(shape) == 3:
            ap = ap.rearrange("p (a b) -> p a b", b=shape[2])
        elif len(shape) == 4:
            ap = ap.rearrange("p (a b c) -> p a b c", b=shape[2], c=shape[3])
        return ap, self.p.tok()


def build_program(n_layers=2, taps=(), stop=None):
    nc = bass.Bass("TRN2", target_bir_lowering=False)
    p = Prog(nc)

    def din(name, shape):
        return nc.dram_tensor(name, shape, F32, kind="ExternalInput").ap()

    h0 = din("h0", [NT, D])
    w_in = din("w_in", [2, D, D_IN])
    w_br = din("w_br", [2, D, D])
    w_out = din("w_out", [2, D, D])
    w_up = din("w_up", [2, D, 2 * DFF])
    w_down = din("w_down", [2, DFF, D])
    nrm = din("nrm", [5, D])
    cst_d = din("cst", [128, NCST])
    c68_d = din("c68", [68, 260])
    mlg_d = din("mlg", [2, 1024])
    dan_d = din("dan", [2, 128])
    dal_d = din("dal", [2, 256])
    y = nc.dram_tensor("y", [2048, D], F32, kind="ExternalOutput").ap()
    tap_out = {}

    def scr(name, shape, dt):
        return nc.dram_tensor(name, shape, dt).ap()

    hA = scr("hA", [NT, D], F32)
    hB = scr("hB", [NT, D], F32)
    s_qT = scr("s_qT", [1024, NT], BF16)
    s_kT = scr("s_kT", [1024, NT], BF16)
    s_k = scr("s_k", [NT, 1024], BF16)
    s_v = scr("s_v", [NT, 1024], BF16)
    s_o = scr("s_o", [NT, 1024], BF16)
    s_gi = scr("s_gi", [4, NT], F32)
    s_gf = scr("s_gf", [4, NT], F32)
    s_dq = scr("s_dq", [512, NT], BF16)
    s_dk = scr("s_dk", [512, NT], BF16)
    s_dv = scr("s_dv", [NT, 1024], BF16)
    s_g = scr("s_g", [6144, NT], BF16)
    s_ff = scr("s_ff", [NTI, 128, 44, 128], BF16)

    cst = nc.alloc_sbuf_tensor("cst_sb", [128, NCST], F32)
    cstb = nc.alloc_sbuf_tensor("cstb", [128, 512], BF16)
    c68 = nc.alloc_sbuf_tensor("c68_sb", [128, 260], F32)
    A_EL = 16 * NT
    A1t = nc.alloc_sbuf_tensor("A1", [128, A_EL], BF16)
    A2t = nc.alloc_sbuf_tensor("A2", [128, A_EL], BF16)
    A3_BYTES = 66560
    A3t = nc.alloc_sbuf_tensor("A3", [128, A3_BYTES // 2], BF16)
    A1 = A1t[:, :].rearrange("p (k t) -> p k t", t=NT)
    A2 = A2t[:, :].rearrange("p (k t) -> p k t", t=NT)
    ps = [nc.alloc_psum_tensor("ps%d" % i, [128, 512], F32) for i in range(8)]
    tps = [Tok(x=True) for _ in range(8)]
    bank = [0]

    def nb():
        b = bank[0]
        bank[0] = (b + 1) % 8
        return b

    t_c = p.tok()
    p.dma("sp", cst[:, :], cst_d, writes=[t_c])
    p.dma("pool", cstb[:, :], cst_d[:, 0:512], writes=[t_c])
    p.dma("sp", c68[0:68, :], c68_d, writes=[t_c])
    epst = nc.alloc_sbuf_tensor("epst", [128, 1], F32)
    p.op("dve", lambda e: e.memset(epst[:, :], EPS), writes=[t_c])
    eps_c = epst[:, 0:1]
    ident_f = cst[:, C_ID:C_ID + 128]
    ident_b = cstb[:, 0:128]
    tri_b = cstb[:, 128:256]
    mv_b = cstb[:, 256:384]
    md_b = cstb[:, 384:512]

    def make_ring(banks):
        st_ = [0]

        def f():
            b = banks[st_[0] % len(banks)]
            st_[0] += 1
            return b
        return f

    def run_interleaved(gens):
        gens = list(gens)
        while gens:
            for g in list(gens):
                try:
                    next(g)
                except StopIteration:
                    gens.remove(g)

    rr = [0]

    def evac_copy(out, in_, reads, writes, scale=None):
        rr[0] ^= 1
        if rr[0]:
            if scale is None:
                p.op("act", lambda e: e.copy(out, in_), reads=reads, writes=writes)
            else:
                p.op("act", lambda e: e.mul(out, in_, scale), reads=reads, writes=writes)
        else:
            if scale is None:
                p.op("dve", lambda e: e.tensor_copy(out, in_), reads=reads, writes=writes)
            else:
                p.op("dve", lambda e: e.tensor_scalar(out, in_, scale, None, ALU.mult), reads=reads, writes=writes)

    def tap(name, src_ap, shape, tok_list, dt=F32):
        if name not in taps:
            return
        o = nc.dram_tensor("tap_" + name, shape, dt, kind="ExternalOutput").ap()
        tap_out[name] = o
        if len(shape) == 2 and shape[1] > 8192:
            n = shape[1]
            for c0 in range(0, n, 8192):
                c1 = min(n, c0 + 8192)
                p.dma("sp", o[:, c0:c1], src_ap[:, c0:c1], reads=tok_list)
        else:
            p.dma("sp", o, src_ap, reads=tok_list)

    def rstd_from_ss(ss, t_ss, n):
        p.op("act", lambda e: e.activation(ss, ss, AF.Ln, bias=eps_c, scale=1.0 / n), reads=[t_ss, t_c], writes=[t_ss])
        p.op("act", lambda e: e.activation(ss, ss, AF.Exp, scale=-0.5), reads=[t_ss], writes=[t_ss])

    def norm_stage(h_src, nrow, dst, t_dst):
        st = Stage(A3t, A3_BYTES, p)
        gB, t_gB = st.alloc([128, D], F32)
        p.dma("sp", gB, nrm[nrow, :].partition_broadcast(128), writes=[t_gB])
        xt = [st.alloc([128, D], F32) for _ in range(4)]
        xn = [st.alloc([128, D], BF16) for _ in range(4)]
        junk, t_junk = st.alloc([128, D], BF16)
        ss = [st.alloc([128, 1], F32) for _ in range(4)]
        nbn = make_ring([0, 1, 2, 3, 4, 5, 6, 7])

        def ntile(i):
            b = i % 4
            x_, tx = xt[b]
            n_, tn_ = xn[b]
            s_, ts = ss[b]
            p.dma("sp", x_, h_src[i * 128:(i + 1) * 128, :], writes=[tx])
            p.op("dve", lambda e: e.memset(s_, 0.0), writes=[ts])
            yield
            p.op("act", lambda e: e.activation(junk, x_, AF.Square, accum_out=s_), reads=[tx], writes=[t_junk, ts])
            yield
            rstd_from_ss(s_, ts, D)
            yield
            p.op("dve", lambda e: e.scalar_tensor_tensor(n_, x_, s_, gB, ALU.mult, ALU.mult),
                 reads=[tx, ts, t_gB], writes=[tn_])
            yield
            for g4 in range(4):
                bk = nbn()
                pbf = ps[bk][:, 0:256].bitcast(BF16)
                for j in range(4):
                    kc = g4 * 4 + j
                    p.op("pe", lambda e: e.transpose(pbf[:, j * 128:(j + 1) * 128], n_[:, kc * 128:(kc + 1) * 128], ident_b),
                         reads=[tn_, t_c], writes=[tps[bk]], inc=(j == 3))
                yield
                evac_copy(dst[:, g4 * 4:(g4 + 1) * 4, i * 128:(i + 1) * 128],
                          pbf.rearrange("p (a b) -> p a b", b=128), [tps[bk]], [t_dst[i // 4]])
                yield
        for i0 in range(0, NTI, 4):
            run_interleaved([ntile(i) for i in range(i0, min(NTI, i0 + 4))])
        p.barrier()

    def load_w(buf, tok, src2d, nkc, ncols, col_off=0):
        p.dma("pool", buf[:, 0:nkc, col_off:col_off + ncols],
              src2d.rearrange("(kc p) n -> p kc n", p=128), writes=[tok])

    def mm_fm(bk, wbuf, tw, c_lo, c_n, act, t_act, kcs, wk0, t0, tn):
        n = len(kcs)
        for i, kc in enumerate(kcs):
            p.op("pe", lambda e: e.matmul(ps[bk][0:c_n, 0:tn], wbuf[:, wk0 + i, c_lo:c_lo + c_n], act[:, kc, t0:t0 + tn],
                                          start=(i == 0), stop=(i == n - 1)),
                 reads=[tw] + t_act, writes=[tps[bk]], inc=(i == n - 1))

    def mm_tm(bk, act, t_act, tile, wbuf, tw, ncols, nkc):
        for kc in range(nkc):
            p.op("pe", lambda e: e.matmul(ps[bk][:, 0:ncols], act[:, kc, tile * 128:(tile + 1) * 128], wbuf[:, kc, 0:ncols],
                                          start=(kc == 0), stop=(kc == nkc - 1)),
                 reads=[tw] + t_act, writes=[tps[bk]], inc=(kc == nkc - 1))

    def zproj_stage(l, tA1, tA2a):
        st = Stage(A3t, A3_BYTES, p)
        wb = [st.alloc([128, 16, 512], BF16) for _ in range(2)]
        ob = [st.alloc([128, NT], BF16) for _ in range(2)]
        tb = [st.alloc([128, 512], BF16) for _ in range(3)]
        trow, t_trow = st.alloc([128, NT + 2], F32)
        axs = [st.alloc([128, 512], F32) for _ in range(2)]
        acc = [st.alloc([128, 512], F32) for _ in range(2)]
        gtmp, t_gtmp = st.alloc([128, 512], F32)
        W = w_in[l]
        wi = [0]
        oi = [0]
        ti = [0]

        def next_wb():
            wi[0] ^= 1
            return wb[wi[0]]

        p.op("dve", lambda e: e.memset(trow[:, 0:2], 0.0), writes=[t_trow])
        q = 0
        for c in (range(4) if 'a' in ZP else []):
            w_, tw = next_wb()
            for s, base in enumerate((0, 512, 1024)):
                load_w(w_, tw, W[:, base + c * 128: base + (c + 1) * 128], 16, 128, col_off=s * 128)
            for (t0, tn) in TGS:
                bx, bc, bb = nb(), nb(), nb()
                tg = [tA1[t0 // 512]]
                mm_fm(bx, w_, tw, 0, 128, A1, tg, range(16), 0, t0, tn)
                mm_fm(bc, w_, tw, 256, 128, A1, tg, range(16), 0, t0, tn)
                mm_fm(bb, w_, tw, 128, 128, A1, tg, range(16), 0, t0, tn)
                q ^= 1
                ax_, tax = axs[q]
                ac_, tac = acc[q]
                p.op("act", lambda e: e.copy(ax_[:, 0:tn], ps[bx][:, 0:tn]), reads=[tps[bx]], writes=[tax])
                p.op("dve", lambda e: e.tensor_tensor(trow[:, 2 + t0:2 + t0 + tn], ps[bc][:, 0:tn], ax_[:, 0:tn], ALU.mult),
                     reads=[tps[bc], tax], writes=[t_trow])

                def cw(j):
                    k = C_CA + (l * 3 + j) * 4 + c
                    return cst[:, k:k + 1]
                p.op("dve", lambda e: e.tensor_scalar(ac_[:, 0:tn], trow[:, t0:t0 + tn], cw(0), None, ALU.mult),
                     reads=[t_trow, t_c], writes=[tac])
                p.op("dve", lambda e: e.scalar_tensor_tensor(ac_[:, 0:tn], trow[:, t0 + 1:t0 + 1 + tn], cw(1), ac_[:, 0:tn], ALU.mult, ALU.add),
                     reads=[t_trow, tac], writes=[tac])
                p.op("dve", lambda e: e.scalar_tensor_tensor(ac_[:, 0:tn], trow[:, t0 + 2:t0 + 2 + tn], cw(2), ac_[:, 0:tn], ALU.mult, ALU.add),
                     reads=[t_trow, tac], writes=[tac])
                p.op("dve", lambda e: e.tensor_tensor(A2[:, c, t0:t0 + tn], ac_[:, 0:tn], ps[bb][:, 0:tn], ALU.mult),
                     reads=[tac, tps[bb]], writes=[tA2a])
        fm_segs = [(1536, 8, s_qT, None, None), (2560, 8, s_kT, 1.0 / 16, None),
                   (5640, 4, s_dq, 1.0 / 8, None), (6152, 4, s_dk, None, None),
                   (7176, 48, s_g, None, AF.Sigmoid)]
        for (c0, nch, dst, scale, func) in (fm_segs if 'fm' in ZP else []):
            for g in range(nch // 4):
                w_, tw = next_wb()
                load_w(w_, tw, W[:, c0 + g * 512: c0 + (g + 1) * 512], 16, 512)
                for ch in range(4):
                    oi[0] ^= 1
                    o_, to = ob[oi[0]]
                    for (t0, tn) in TGS:
                        bk = nb()
                        mm_fm(bk, w_, tw, ch * 128, 128, A1, [tA1[t0 // 512]], range(16), 0, t0, tn)
                        if func is not None:
                            p.op("act", lambda e: e.activation(o_[:, t0:t0 + tn], ps[bk][:, 0:tn], func),
                                 reads=[tps[bk]], writes=[to])
                        else:
                            evac_copy(o_[:, t0:t0 + tn], ps[bk][:, 0:tn], [tps[bk]], [to], scale=scale)
                    r0 = (g * 4 + ch) * 128
                    p.dma("sp", dst[r0:r0 + 128, :], o_, reads=[to])
        tm_segs = [(2560, 1024, s_k, 1.0 / 16, None), (3584, 1024, s_v, None, None),
                   (4608, 1024, s_o, None, AF.Sigmoid), (6664, 512, s_dv, None, None)]
        if 'tm1' in ZP:
            tm_segs = tm_segs[3:4]
        for (c0, ncol, dst, scale, func) in (tm_segs if ('tm' in ZP or 'tm1' in ZP) else []):
            for g in range(ncol // 512):
                w_, tw = next_wb()
                load_w(w_, tw, W[:, c0 + g * 512: c0 + (g + 1) * 512], 16, 512)
                for i in range(NTI):
                    bk = nb()
                    mm_tm(bk, A1, [tA1[i // 4]], i, w_, tw, 512, 16)
                    ti[0] = (ti[0] + 1) % 3
                    t_, tt = tb[ti[0]]
                    if func is not None:
                        p.op("act", lambda e: e.activation(t_, ps[bk][:, :], func), reads=[tps[bk]], writes=[tt])
                    else:
                        evac_copy(t_, ps[bk][:, :], [tps[bk]], [tt], scale=scale)
                    p.dma("sp", dst[i * 128:(i + 1) * 128, g * 512:(g + 1) * 512], t_, reads=[tt])
        w_, tw = next_wb()
        load_w(w_, tw, W[:, 5632:5640], 16, 8)
        for which, dst in (((0, s_gi), (1, s_gf)) if 'if' in ZP else []):
            for (t0, tn) in TGS:
                bk = nb()
                mm_fm(bk, w_, tw, which * 4, 4, A1, [tA1[t0 // 512]], range(16), 0, t0, tn)
                p.op("dve", lambda e: e.tensor_copy(gtmp[0:4, 0:tn], ps[bk][0:4, 0:tn]), reads=[tps[bk]], writes=[t_gtmp])
                p.dma("sp", dst[:, t0:t0 + tn], gtmp[0:4, 0:tn], reads=[t_gtmp])
        p.barrier()

    def mlstm_stage(l, tA2m):
        st = Stage(A3t, A3_BYTES, p)

        def g68():
            return st.alloc([128, 128], F32)
        gi, t_gi = g68()
        gf, t_gf = g68()
        lf, t_lf = g68()
        bb, t_bb = g68()
        aa, t_aa = g68()
        wk, t_wk = g68()
        ee, t_ee = g68()
        onesm, t_on = g68()
        amax, t_amax = st.alloc([128, 1], F32)
        negRc, t_negRc = st.alloc([128, 1], F32)
        rows = [st.alloc([128, 68], F32) for _ in range(7)]
        (rowA, t_rA), (rowB, t_rB), (mnext, t_mn), (mprev, t_mp), (Rrow, t_R), (negR, t_nR), (gsrow, t_gs) = rows
        wkcol, t_wkc = st.alloc([128, 68], F32)
        ecol, t_ec = st.alloc([128, 68], F32)
        gsB, t_gsB = st.alloc([128, 68], F32)
        mlgB, t_mlg = st.alloc([128, 1024], F32)
        C, t_C = st.alloc([128, 8, 257], F32)
        Cd, t_Cd = st.alloc([128, 8, 257], F32)
        Cb, t_Cb = st.alloc([128, 8, 257], BF16)
        qT = [st.alloc([128, 8, 128], BF16) for _ in range(2)]
        kT = [st.alloc([128, 8, 128], BF16) for _ in range(2)]
        kt = [st.alloc([128, 1024], BF16) for _ in range(2)]
        va = [st.alloc([128, 4, 257], BF16) for _ in range(2)]
        ot = [st.alloc([128, 1024], BF16) for _ in range(2)]
        Pb = [st.alloc([128, 128], BF16) for _ in range(4)]
        vw = [st.alloc([128, 257], BF16) for _ in range(4)]
        hn = [st.alloc([128, 256], F32) for _ in range(4)]
        t_Ch, t_Cdh, t_Cbh = p.toks(4), p.toks(4), p.toks(4)
        hm = [st.alloc([128, 1024], BF16) for _ in range(2)]
        junk, t_junk = st.alloc([128, 256], BF16)
        sm = [st.alloc([128, 4], F32) for _ in range(4)]

        p.dma("sp", gi[0:68, :], s_gi.rearrange("h (c t) -> (h c) t", t=128), writes=[t_gi])
        p.dma("sp", gf[0:68, :], s_gf.rearrange("h (c t) -> (h c) t", t=128), writes=[t_gf])
        p.dma("sp", mlgB, mlg_d[l, :].partition_broadcast(128), writes=[t_mlg])
        bi = c68[0:68, 2 * l:2 * l + 1]
        bf_ = c68[0:68, 2 * l + 1:2 * l + 2]
        padneg = c68[0:68, 4:132]
        valid = c68[0:68, 132:260]
        S = slice(0, 68)
        p.op("dve", lambda e: e.tensor_scalar(aa[S, :], gi[S, :], bi, None, ALU.add), reads=[t_gi, t_c], writes=[t_aa])
        p.op("act", lambda e: e.activation(lf[S, :], gf[S, :], AF.Sigmoid, bias=bf_), reads=[t_gf, t_c], writes=[t_lf])
        p.op("act", lambda e: e.activation(lf[S, :], lf[S, :], AF.Ln), reads=[t_lf], writes=[t_lf])
        p.op("dve", lambda e: e.tensor_tensor(lf[S, :], lf[S, :], valid, ALU.mult), reads=[t_lf, t_c], writes=[t_lf])
        p.op("dve", lambda e: e.memset(onesm[:, :], 1.0), writes=[t_on])
        p.op("dve", lambda e: e.tensor_tensor_scan(bb[S, :], onesm[S, :], lf[S, :], 0.0, ALU.mult, ALU.add),
             reads=[t_on, t_lf], writes=[t_bb])
        p.op("dve", lambda e: e.tensor_tensor(aa[S, :], aa[S, :], bb[S, :], ALU.subtract), reads=[t_aa, t_bb], writes=[t_aa])
        p.op("dve", lambda e: e.tensor_tensor(aa[S, :], aa[S, :], padneg, ALU.add), reads=[t_aa, t_c], writes=[t_aa])
        p.op("dve", lambda e: e.reduce_max(amax[S, :], aa[S, :], AX.X), reads=[t_aa], writes=[t_amax])
        bk = nb()
        p.op("pe", lambda e: e.matmul(ps[bk][0:1, 0:68], amax[S, 0:1], ident_f[0:68, 0:68], start=True, stop=True),
             reads=[t_amax, t_c], writes=[tps[bk]])
        p.op("dve", lambda e: e.tensor_copy(rowA[0:1, :], ps[bk][0:1, 0:68]), reads=[tps[bk]], writes=[t_rA])
        bk = nb()
        p.op("pe", lambda e: e.matmul(ps[bk][0:1, 0:68], bb[S, 127:128], ident_f[0:68, 0:68], start=True, stop=True),
             reads=[t_bb, t_c], writes=[tps[bk]])
        p.op("dve", lambda e: e.tensor_copy(rowB[0:1, :], ps[bk][0:1, 0:68]), reads=[tps[bk]], writes=[t_rB])
        for h in range(4):
            sl = slice(h * 17, (h + 1) * 17)
            p.op("dve", lambda e: e.tensor_tensor_scan(mnext[0:1, sl], rowA[0:1, sl], rowB[0:1, sl], 0.0, ALU.max, ALU.add),
                 reads=[t_rA, t_rB], writes=[t_mn])
        p.op("dve", lambda e: e.memset(mprev[0:1, :], 0.0), writes=[t_mp])
        for h in range(4):
            p.op("dve", lambda e: e.tensor_copy(mprev[0:1, h * 17 + 1:h * 17 + 17], mnext[0:1, h * 17:h * 17 + 16]),
                 reads=[t_mn], writes=[t_mp])
        p.op("dve", lambda e: e.tensor_tensor(Rrow[0:1, :], mprev[0:1, :], rowA[0:1, :], ALU.max), reads=[t_mp, t_rA], writes=[t_R])
        p.op("dve", lambda e: e.tensor_scalar(negR[0:1, :], Rrow[0:1, :], -1.0, None, ALU.mult), reads=[t_R], writes=[t_nR])
        p.op("dve", lambda e: e.tensor_tensor(gsrow[0:1, :], mprev[0:1, :], Rrow[0:1, :], ALU.subtract), reads=[t_mp, t_R], writes=[t_gs])
        p.op("act", lambda e: e.activation(gsrow[0:1, :], gsrow[0:1, :], AF.Exp), reads=[t_gs], writes=[t_gs])
        bk = nb()
        p.op("pe", lambda e: e.matmul(ps[bk][0:68, 0:1], negR[0:1, :], ident_f[0:1, 0:1], start=True, stop=True),
             reads=[t_nR, t_c], writes=[tps[bk]])
        p.op("dve", lambda e: e.tensor_copy(negRc[S, :], ps[bk][0:68, 0:1]), reads=[tps[bk]], writes=[t_negRc])
        p.op("act", lambda e: e.activation(wk[S, :], aa[S, :], AF.Exp, bias=negRc[S, 0:1]), reads=[t_aa, t_negRc], writes=[t_wk])
        p.op("act", lambda e: e.activation(ee[S, :], bb[S, :], AF.Exp, bias=negRc[S, 0:1], scale=-1.0), reads=[t_bb, t_negRc], writes=[t_ee])
        for src, tsrc, dst, tdst in ((wk, t_wk, wkcol, t_wkc), (ee, t_ee, ecol, t_ec)):
            bk = nb()
            p.op("pe", lambda e: e.matmul(ps[bk][:, 0:68], src[S, :], ident_f[0:68, 0:68], start=True, stop=True),
                 reads=[tsrc, t_c], writes=[tps[bk]])
            p.op("dve", lambda e: e.tensor_copy(dst[:, :], ps[bk][:, 0:68]), reads=[tps[bk]], writes=[tdst])
        bk = nb()
        p.op("pe", lambda e: e.matmul(ps[bk][:, 0:68], onesm[0:1, :], gsrow[0:1, :], start=True, stop=True),
             reads=[t_on, t_gs], writes=[tps[bk]])
        p.op("dve", lambda e: e.tensor_copy(gsB[:, :], ps[bk][:, 0:68]), reads=[tps[bk]], writes=[t_gsB])
        tap("wkcol%d" % l, wkcol, [128, 68], [t_wkc])
        tap("ecol%d" % l, ecol, [128, 68], [t_ec])
        tap("gsB%d" % l, gsB, [128, 68], [t_gsB])

        p.op("dve", lambda e: e.memset(C[:, :, :], 0.0), writes=t_Ch)
        for r in range(2):
            p.op("dve", lambda e: e.memset(va[r][0][:, :, 256:257], 1.0), writes=[va[r][1]])
        nbs = make_ring([4, 5, 6, 7])
        qTs = s_qT.rearrange("(c p) t -> p c t", p=128)
        kTs = s_kT.rearrange("(c p) t -> p c t", p=128)
        for i in range(NTI):
            r = i % 2
            cols = slice(i * 128, (i + 1) * 128)
            q_, tq = qT[r]
            kT_, tkT = kT[r]
            k_, tk = kt[r]
            v_, tv = va[r]
            o_, to = ot[r]
            hm_, thm = hm[r]
            p.dma("sp", q_, qTs[:, :, cols], writes=[tq])
            p.dma("sp", kT_, kTs[:, :, cols], writes=[tkT])
            p.dma("sp", k_, s_k[cols, :], writes=[tk])
            p.dma("sp", v_[:, :, 0:256], s_v[cols, :].rearrange("p (h e) -> p h e", e=256), writes=[tv])
            p.dma("sp", o_, s_o[cols, :], writes=[to])
            def unit(h):
                idx = 17 * h + i
                P_, tP = Pb[h]
                vw_, tvw = vw[h]
                hn_, thn = hn[h]
                sm_, tsm = sm[h]
                hs = slice(2 * h, 2 * h + 2)
                bs = nbs()
                for half in range(2):
                    p.op("pe", lambda e: e.matmul(ps[bs][:, 0:128], kT_[:, 2 * h + half, :], q_[:, 2 * h + half, :],
                                                  start=(half == 0), stop=(half == 1)),
                         reads=[tkT, tq], writes=[tps[bs]], inc=(half == 1))
                yield
                p.op("dve", lambda e: e.tensor_tensor(P_, ps[bs][:, 0:128], tri_b, ALU.mult), reads=[tps[bs], t_c], writes=[tP])
                yield
                p.op("dve", lambda e: e.tensor_scalar(Cd[:, hs, :], C[:, hs, :], gsB[:, idx:idx + 1], None, ALU.mult),
                     reads=[t_Ch[h], t_gsB], writes=[t_Cdh[h]])
                yield
                p.op("act", lambda e: e.copy(Cb[:, hs, :], Cd[:, hs, :]), reads=[t_Cdh[h]], writes=[t_Cbh[h]])
                yield
                p.op("act", lambda e: e.activation(vw_, v_[:, h, :], AF.Copy, scale=wkcol[:, idx:idx + 1]),
                     reads=[tv, t_wkc], writes=[tvw])
                yield
                bn = h
                for half in range(2):
                    p.op("pe", lambda e: e.matmul(ps[bn][:, 0:257], q_[:, 2 * h + half, :], Cb[:, 2 * h + half, :],
                                                  start=(half == 0), stop=False),
                         reads=[tq, t_Cbh[h]], writes=[tps[bn]], inc=False)
                p.op("pe", lambda e: e.matmul(ps[bn][:, 0:257], P_, vw_, start=False, stop=True),
                     reads=[tP, tvw], writes=[tps[bn]])
                yield
                for half in range(2):
                    bc = nbs()
                    p.op("pe", lambda e: e.matmul(ps[bc][:, 0:257], k_[:, h * 256 + half * 128:h * 256 + (half + 1) * 128], vw_,
                                                  start=True, stop=True),
                         reads=[tk, tvw], writes=[tps[bc]])
                    yield
                    p.op("dve", lambda e: e.tensor_tensor(C[:, 2 * h + half, :], Cd[:, 2 * h + half, :], ps[bc][:, 0:257], ALU.add),
                         reads=[t_Cdh[h], tps[bc]], writes=[t_Ch[h]])
                    yield
                dn = sm_[:, 0:1]
                rc = sm_[:, 1:2]
                ssq = sm_[:, 2:3]
                r2 = sm_[:, 3:4]
                p.op("dve", lambda e: e.tensor_copy(dn, ps[bn][:, 256:257]), reads=[tps[bn]], writes=[tsm])
                p.op("dve", lambda e: e.scalar_tensor_tensor(dn, dn, -1.0, dn, ALU.mult, ALU.max), reads=[tsm], writes=[tsm])
                p.op("dve", lambda e: e.tensor_tensor(dn, dn, ecol[:, idx:idx + 1], ALU.max), reads=[tsm, t_ec], writes=[tsm])
                yield
                p.op("dve", lambda e: e.reciprocal(rc, dn), reads=[tsm], writes=[tsm])
                p.op("dve", lambda e: e.memset(ssq, 0.0), reads=[tsm], writes=[tsm])
                yield
                p.op("act", lambda e: e.activation(junk, ps[bn][:, 0:256], AF.Square, scale=rc, accum_out=ssq),
                     reads=[tps[bn], tsm], writes=[t_junk, tsm])
                yield
                rstd_from_ss(ssq, tsm, 256)
                yield
                p.op("dve", lambda e: e.tensor_tensor(r2, ssq, rc, ALU.mult), reads=[tsm], writes=[tsm])
                yield
                p.op("dve", lambda e: e.scalar_tensor_tensor(hn_, ps[bn][:, 0:256], r2, mlgB[:, h * 256:(h + 1) * 256], ALU.mult, ALU.mult),
                     reads=[tps[bn], tsm, t_mlg], writes=[thn])
                yield
                p.op("dve", lambda e: e.tensor_tensor(hm_[:, h * 256:(h + 1) * 256], hn_, o_[:, h * 256:(h + 1) * 256], ALU.mult),
                     reads=[thn, to], writes=[thm])
            run_interleaved([unit(h) for h in range(4)])
            for g2 in range(2):
                bk = nb()
                pbf = ps[bk][:, 0:256].bitcast(BF16)
                for j in range(4):
                    cc = g2 * 4 + j
                    p.op("pe", lambda e: e.transpose(pbf[:, j * 128:(j + 1) * 128], hm_[:, cc * 128:(cc + 1) * 128], ident_b),
                         reads=[thm, t_c], writes=[tps[bk]], inc=(j == 3))
                evac_copy(A2[:, 4 + g2 * 4:4 + (g2 + 1) * 4, cols], pbf.rearrange("p (a b) -> p a b", b=128), [tps[bk]], [tA2m])
        p.barrier()

    def attn_stage(l, tA2d):
        st = Stage(A3t, A3_BYTES, p)
        dq = A1[:, 0:4, :]
        dk = A1[:, 4:8, :]
        dv = A1t[:, 8 * NT:8 * NT + NTI * 512].rearrange("p (i e) -> p i e", e=512)
        t_qkv = p.tok()
        p.dma("sp", dq, s_dq.rearrange("(h p) t -> p h t", p=128), writes=[t_qkv])
        p.dma("sp", dk, s_dk.rearrange("(h p) t -> p h t", p=128), writes=[t_qkv])
        p.dma("sp", dv, s_dv[:, 0:512].rearrange("(i p) e -> p i e", p=128), writes=[t_qkv])
        sc = [st.alloc([128, NT], F32) for _ in range(2)]
        PmA = A1t[:, 8 * NT + NTI * 512:16 * NT].rearrange("p (a t) -> p a t", t=NT)
        Pm = [[(PmA[:, 2 * s_ + m_, :], p.tok()) for m_ in range(2)] for s_ in range(2)]
        Aa = [st.alloc([128, NT], BF16) for _ in range(2)]
        tmpA = [st.alloc([128, NT], BF16) for _ in range(2)]
        AT = [st.alloc([128, NTI, 128], BF16) for _ in range(2)]
        hd = [st.alloc([128, 512], BF16) for _ in range(2)]
        dalB, t_dal = st.alloc([128, 256], F32)
        danS, t_dan = st.alloc([128, 128], F32)
        junk, t_junk = st.alloc([128, 128], BF16)
        lam, t_lam = st.alloc([128, 4], F32)
        mx = [[st.alloc([128, 8], F32) for _ in range(2)] for _ in range(2)]
        sm = [st.alloc([128, 8], F32) for _ in range(2)]
        lam_init = 0.8 - 0.6 * math.exp(-0.3 * l)
        p.dma("sp", dalB, dal_d[l, :].partition_broadcast(128), writes=[t_dal])
        p.dma("sp", danS, dan_d[l, :].partition_broadcast(128), writes=[t_dan])
        p.op("dve", lambda e: e.tensor_scalar(danS, danS, 1.0 - lam_init, None, ALU.mult), reads=[t_dan], writes=[t_dan])
        p.op("dve", lambda e: e.tensor_tensor(dalB[:, 0:64], dalB[:, 0:64], dalB[:, 64:128], ALU.mult), reads=[t_dal], writes=[t_dal])
        p.op("dve", lambda e: e.tensor_tensor(dalB[:, 128:192], dalB[:, 128:192], dalB[:, 192:256], ALU.mult), reads=[t_dal], writes=[t_dal])
        p.op("dve", lambda e: e.reduce_sum(lam[:, 0:1], dalB[:, 0:64], AX.X), reads=[t_dal], writes=[t_lam])
        p.op("dve", lambda e: e.reduce_sum(lam[:, 1:2], dalB[:, 128:192], AX.X), reads=[t_dal], writes=[t_lam])
        p.op("act", lambda e: e.activation(lam[:, 0:2], lam[:, 0:2], AF.Exp), reads=[t_lam], writes=[t_lam])
        p.op("dve", lambda e: e.tensor_tensor(lam[:, 2:3], lam[:, 0:1], lam[:, 1:2], ALU.subtract), reads=[t_lam], writes=[t_lam])
        p.op("dve", lambda e: e.tensor_scalar(lam[:, 2:3], lam[:, 2:3], lam_init, -1.0, ALU.add, ALU.mult), reads=[t_lam], writes=[t_lam])
        nlam = lam[:, 2:3]
        nba = make_ring([2, 3, 4, 5, 6, 7])
        for j in range(NTI):
            kend = (j + 1) * 128
            qcols = slice(j * 128, (j + 1) * 128)
            hd_, thd = hd[j % 2]

            def aunit(h, sl):
                sm_, tsm = sm[sl]
                AT_, tAT = AT[sl]
                sc_, tsc = sc[sl]
                A_, tA = Aa[sl]
                tm_, ttm = tmpA[sl]
                for m in range(2):
                    Pm_, tPm = Pm[sl][m]
                    mx_, tmx = mx[sl][m]
                    rs = slice(m * 64, (m + 1) * 64)
                    nblk5 = (kend + 511) // 512
                    for b5 in range(nblk5):
                        k0 = b5 * 512
                        kn = min(512, kend - k0)
                        masks = []
                        if k0 == 0:
                            masks.append((0, mv_b))
                        if j >= 1 and k0 <= j * 128 < k0 + kn:
                            masks.append((j * 128 - k0, md_b))
                        bk = nba()
                        p.op("pe", lambda e: e.matmul(ps[bk][:, 0:kn], dq[rs, h, qcols], dk[rs, h, k0:k0 + kn], start=True, stop=(len(masks) == 0)),
                             reads=[t_qkv], writes=[tps[bk]], inc=(len(masks) == 0))
                        for mi, (co, mk) in enumerate(masks):
                            p.op("pe", lambda e: e.matmul(ps[bk][:, co:co + 128], ident_b, mk, start=False, stop=(mi == len(masks) - 1)),
                                 reads=[t_c], writes=[tps[bk]], inc=(mi == len(masks) - 1))
                        yield
                        p.op("dve", lambda e: e.reduce_max(mx_[:, b5:b5 + 1], ps[bk][:, 0:kn], AX.X), reads=[tps[bk]], writes=[tmx])
                        yield
                        p.op("act", lambda e: e.copy(sc_[:, k0:k0 + kn], ps[bk][:, 0:kn]), reads=[tps[bk]], writes=[tsc])
                        yield
                    p.op("dve", lambda e: e.reduce_max(mx_[:, 6:7], mx_[:, 0:nblk5], AX.X), reads=[tmx], writes=[tmx])
                    p.op("dve", lambda e: e.tensor_scalar(mx_[:, 7:8], mx_[:, 6:7], -1.0, None, ALU.mult), reads=[tmx], writes=[tmx])
                    p.op("dve", lambda e: e.memset(sm_[:, m:m + 1], 0.0), writes=[tsm])
                    yield
                    p.op("act", lambda e: e.activation(Pm_[:, 0:kend], sc_[:, 0:kend], AF.Exp, bias=mx_[:, 7:8], accum_out=sm_[:, m:m + 1]),
                         reads=[tsc, tmx, tsm], writes=[tPm, tsm])
                    yield
                p.op("dve", lambda e: e.reciprocal(sm_[:, 2:4], sm_[:, 0:2]), reads=[tsm], writes=[tsm])
                p.op("dve", lambda e: e.tensor_tensor(sm_[:, 4:5], sm_[:, 3:4], nlam, ALU.mult), reads=[tsm, t_lam], writes=[tsm])
                yield
                p.op("act", lambda e: e.activation(tm_[:, 0:kend], Pm[sl][0][0][:, 0:kend], AF.Copy, scale=sm_[:, 2:3]),
                     reads=[Pm[sl][0][1], tsm], writes=[ttm])
                yield
                p.op("dve", lambda e: e.scalar_tensor_tensor(A_[:, 0:kend], Pm[sl][1][0][:, 0:kend], sm_[:, 4:5], tm_[:, 0:kend], ALU.mult, ALU.add),
                     reads=[Pm[sl][1][1], tsm, ttm], writes=[tA])
                yield
                nk = j + 1
                for g4 in range((nk + 3) // 4):
                    bk = nba()
                    pbf = ps[bk][:, 0:256].bitcast(BF16)
                    n4 = min(4, nk - g4 * 4)
                    for jj in range(n4):
                        kb = g4 * 4 + jj
                        p.op("pe", lambda e: e.transpose(pbf[:, jj * 128:(jj + 1) * 128], A_[:, kb * 128:(kb + 1) * 128], ident_b),
                             reads=[tA, t_c], writes=[tps[bk]], inc=(jj == n4 - 1))
                    yield
                    evac_copy(AT_[:, g4 * 4:g4 * 4 + n4, :], pbf[:, 0:n4 * 128].rearrange("p (a b) -> p a b", b=128), [tps[bk]], [tAT])
                    yield
                bo = sl
                for kb in range(nk):
                    p.op("pe", lambda e: e.matmul(ps[bo][:, 0:128], AT_[:, kb, :], dv[:, kb, h * 128:(h + 1) * 128],
                                                  start=(kb == 0), stop=(kb == nk - 1)),
                         reads=[tAT, t_qkv], writes=[tps[bo]], inc=(kb == nk - 1))
                yield
                ssq = sm_[:, 5:6]
                p.op("dve", lambda e: e.memset(ssq, 0.0), reads=[tsm], writes=[tsm])
                yield
                p.op("act", lambda e: e.activation(junk, ps[bo][:, 0:128], AF.Square, accum_out=ssq), reads=[tps[bo], tsm], writes=[t_junk, tsm])
                yield
                rstd_from_ss(ssq, tsm, 128)
                yield
                p.op("dve", lambda e: e.scalar_tensor_tensor(hd_[:, h * 128:(h + 1) * 128], ps[bo][:, 0:128], ssq, danS, ALU.mult, ALU.mult),
                     reads=[tps[bo], tsm, t_dan], writes=[thd])
            run_interleaved([aunit(0, 0), aunit(1, 1)])
            run_interleaved([aunit(2, 0), aunit(3, 1)])
            if j == 0:
                p.op("dve", lambda e: e.memset(hd_[0:112, :], 0.0), writes=[thd])
            bk = nb()
            pbf = ps[bk][:, 0:256].bitcast(BF16)
            for h in range(4):
                p.op("pe", lambda e: e.transpose(pbf[:, h * 128:(h + 1) * 128], hd_[:, h * 128:(h + 1) * 128], ident_b),
                     reads=[thd, t_c], writes=[tps[bk]], inc=(h == 3))
            evac_copy(A2[:, 12:16, qcols], pbf.rearrange("p (a b) -> p a b", b=128), [tps[bk]], [tA2d])
        p.barrier()

    def merge_stage(l, tA2, tA1):
        st = Stage(A3t, A3_BYTES, p)
        wb = [st.alloc([128, 16, 512], BF16) for _ in range(2)]
        sg = [st.alloc([128, 3, NT], BF16) for _ in range(2)]
        m1, t_m1 = st.alloc([128, 512], F32)
        m2, t_m2 = st.alloc([128, 512], F32)
        m3, t_m3 = st.alloc([128, 512], F32)
        sgs = s_g.rearrange("(b j p) t -> p b j t", b=3, p=128)
        load_w(wb[0][0], wb[0][1], w_br[l][:, 0:512], 16, 512)
        for g in range(4):
            w_, tw = wb[g % 2]
            if g + 1 < 4:
                load_w(wb[(g + 1) % 2][0], wb[(g + 1) % 2][1], w_br[l][:, (g + 1) * 512:(g + 2) * 512], 16, 512)
            for jj in range(4):
                j = g * 4 + jj
                s_, ts = sg[j % 2]
                p.dma("sp", s_, sgs[:, :, j, :], writes=[ts])
                for (t0, tn) in TGS:
                    ba, bm, bd = nb(), nb(), nb()
                    mm_fm(ba, w_, tw, jj * 128, 128, A2, tA2, range(0, 4), 0, t0, tn)
                    mm_fm(bm, w_, tw, jj * 128, 128, A2, tA2, range(4, 12), 4, t0, tn)
                    mm_fm(bd, w_, tw, jj * 128, 128, A2, tA2, range(12, 16), 12, t0, tn)
                    p.op("dve", lambda e: e.tensor_tensor(m1[:, 0:tn], ps[ba][:, 0:tn], s_[:, 0, t0:t0 + tn], ALU.mult),
                         reads=[tps[ba], ts], writes=[t_m1])
                    p.op("dve", lambda e: e.tensor_tensor(m2[:, 0:tn], ps[bm][:, 0:tn], s_[:, 1, t0:t0 + tn], ALU.mult),
                         reads=[tps[bm], ts], writes=[t_m2])
                    p.op("dve", lambda e: e.tensor_tensor(m3[:, 0:tn], ps[bd][:, 0:tn], s_[:, 2, t0:t0 + tn], ALU.mult),
                         reads=[tps[bd], ts], writes=[t_m3])
                    p.op("dve", lambda e: e.tensor_tensor(m1[:, 0:tn], m1[:, 0:tn], m2[:, 0:tn], ALU.add),
                         reads=[t_m1, t_m2], writes=[t_m1])
                    p.op("dve", lambda e: e.tensor_tensor(A1[:, j, t0:t0 + tn], m1[:, 0:tn], m3[:, 0:tn], ALU.add),
                         reads=[t_m1, t_m3], writes=[tA1[t0 // 512]])
        p.barrier()

    def wout_stage(l, h_src, h_dst, tA1):
        st = Stage(A3t, A3_BYTES, p)
        wb = [st.alloc([128, 16, 512], BF16) for _ in range(2)]
        hx = [st.alloc([128, 512], F32) for _ in range(4)]
        hn = [st.alloc([128, 512], F32) for _ in range(4)]
        items = [(cg, i) for cg in range(4) for i in range(NTI)]

        def issue_load(n):
            cg, i = items[n]
            x_, tx = hx[n % 4]
            p.dma("sp", x_, h_src[i * 128:(i + 1) * 128, cg * 512:(cg + 1) * 512], writes=[tx])
        PF = 3
        for n in range(PF):
            issue_load(n)
        load_w(wb[0][0], wb[0][1], w_out[l][:, 0:512], 16, 512)
        for n, (cg, i) in enumerate(items):
            w_, tw = wb[cg % 2]
            if i == 0 and cg + 1 < 4:
                load_w(wb[(cg + 1) % 2][0], wb[(cg + 1) % 2][1], w_out[l][:, (cg + 1) * 512:(cg + 2) * 512], 16, 512)
            if n + PF < len(items):
                issue_load(n + PF)
            x_, tx = hx[n % 4]
            n_, tn_ = hn[n % 4]
            rows = slice(i * 128, (i + 1) * 128)
            ccols = slice(cg * 512, (cg + 1) * 512)
            bk = nb()
            mm_tm(bk, A1, [tA1[i // 4]], i, w_, tw, 512, 16)
            p.op("dve", lambda e: e.tensor_tensor(n_, ps[bk][:, :], x_, ALU.add), reads=[tps[bk], tx], writes=[tn_])
            p.dma("sp", h_dst[rows, ccols], n_, reads=[tn_])
        p.barrier()

    def ffn_up_stage(l, tA1, wd):
        st = Stage(A3t, A3_BYTES, p)
        wa = [st.alloc([128, 16, 256], BF16) for _ in range(2)]
        wg = [st.alloc([128, 16, 256], BF16) for _ in range(2)]
        arow = [st.alloc([128, NT + 2], F32) for _ in range(2)]
        acc = [st.alloc([128, 512], F32) for _ in range(2)]
        ge, t_ge = st.alloc([128, 512], F32)
        ob = [st.alloc([128, NT], BF16) for _ in range(2)]
        for r in range(2):
            p.op("dve", lambda e: e.memset(arow[r][0][:, 0:2], 0.0), writes=[arow[r][1]])
        W = w_up[l]
        ffd = s_ff.rearrange("i p f t -> p i f t")
        q = 0
        def load_grp(grp):
            load_w(wa[grp % 2][0], wa[grp % 2][1], W[:, grp * 256:(grp + 1) * 256], 16, 256)
            load_w(wg[grp % 2][0], wg[grp % 2][1], W[:, DFF + grp * 256:DFF + (grp + 1) * 256], 16, 256)
        load_grp(0)
        for grp in range(22):
            wa_, twa = wa[grp % 2]
            wg_, twg = wg[grp % 2]
            if grp + 1 < 22:
                load_grp(grp + 1)
            if grp == 2:
                load_wd(l, wd, 0)
            for cc in range(2):
                f = grp * 2 + cc
                ar, tar = arow[f % 2]
                o_, to = ob[f % 2]
                for (t0, tn) in TGS:
                    ba, bb = nb(), nb()
                    tg = [tA1[t0 // 512]]
                    mm_fm(ba, wa_, twa, cc * 128, 128, A1, tg, range(16), 0, t0, tn)
                    mm_fm(bb, wg_, twg, cc * 128, 128, A1, tg, range(16), 0, t0, tn)
                    q ^= 1
                    ac_, tac = acc[q]
                    p.op("act", lambda e: e.copy(ar[:, 2 + t0:2 + t0 + tn], ps[ba][:, 0:tn]), reads=[tps[ba]], writes=[tar])

                    def cw(j):
                        k = C_CF + (l * 3 + j) * 44 + f
                        return cst[:, k:k + 1]
                    kb = C_FB + l * 44 + f
                    p.op("act", lambda e: e.activation(ac_[:, 0:tn], ar[:, t0:t0 + tn], AF.Identity, bias=cst[:, kb:kb + 1], scale=cw(0)),
                         reads=[tar, t_c], writes=[tac])
                    p.op("dve", lambda e: e.scalar_tensor_tensor(ac_[:, 0:tn], ar[:, t0 + 1:t0 + 1 + tn], cw(1), ac_[:, 0:tn], ALU.mult, ALU.add),
                         reads=[tar, tac], writes=[tac])
                    p.op("dve", lambda e: e.scalar_tensor_tensor(ac_[:, 0:tn], ar[:, t0 + 2:t0 + 2 + tn], cw(2), ac_[:, 0:tn], ALU.mult, ALU.add),
                         reads=[tar, tac], writes=[tac])
                    p.op("act", lambda e: e.activation(ge[:, 0:tn], ac_[:, 0:tn], AF.Gelu), reads=[tac], writes=[t_ge])
                    p.op("dve", lambda e: e.tensor_tensor(o_[:, t0:t0 + tn], ge[:, 0:tn], ps[bb][:, 0:tn], ALU.mult),
                         reads=[t_ge, tps[bb]], writes=[to])
                p.dma("sp", ffd[:, :, f, :], o_.rearrange("p (i t) -> p i t", t=128), reads=[to])
        p.barrier()

    def wd_views():
        return [(A2t[:, 0:44 * 512].rearrange("p (k n) -> p k n", n=512), p.tok()),
                (A1t[:, 0:44 * 512].rearrange("p (k n) -> p k n", n=512), p.tok())]

    def load_wd(l, wd, cg):
        w_, tw = wd[cg % 2]
        for hf in range(2):
            p.dma("pool", w_[:, hf * 22:(hf + 1) * 22, :],
                  w_down[l][hf * 2816:(hf + 1) * 2816, cg * 512:(cg + 1) * 512].rearrange("(kc p) n -> p kc n", p=128), writes=[tw])

    def ffn_down_stage(l, h_src, h_dst, wd):
        st = Stage(A3t, A3_BYTES, p)
        gt = [st.alloc([128, 44, 128], BF16) for _ in range(3)]
        hx = [st.alloc([128, 512], F32) for _ in range(3)]
        hn = [st.alloc([128, 512], F32) for _ in range(3)]
        items = [(cg, i) for cg in range(4) for i in range(NTI)]

        def issue_load(n):
            cg, i = items[n]
            g_, tg = gt[n % 3]
            x_, tx = hx[n % 3]
            p.dma("sp", g_, s_ff[i], writes=[tg])
            p.dma("sp", x_, h_src[i * 128:(i + 1) * 128, cg * 512:(cg + 1) * 512], writes=[tx])

        PF = 2
        for n in range(PF):
            issue_load(n)
        for n, (cg, i) in enumerate(items):
            w_, tw = wd[cg % 2]
            if i == 0 and cg + 1 < 4:
                load_wd(l, wd, cg + 1)
            if n + PF < len(items):
                issue_load(n + PF)
            g_, tg = gt[n % 3]
            x_, tx = hx[n % 3]
            n_, tn_ = hn[n % 3]
            rows = slice(i * 128, (i + 1) * 128)
            ccols = slice(cg * 512, (cg + 1) * 512)
            bk = nb()
            for kc in range(44):
                p.op("pe", lambda e: e.matmul(ps[bk][:, :], g_[:, kc, :], w_[:, kc, :], start=(kc == 0), stop=(kc == 43)),
                     reads=[tg, tw], writes=[tps[bk]], inc=(kc == 43))
            p.op("dve", lambda e: e.tensor_tensor(n_, ps[bk][:, :], x_, ALU.add), reads=[tps[bk], tx], writes=[tn_])
            p.dma("sp", h_dst[rows, ccols], n_, reads=[tn_])
        p.barrier()

    def final_stage(h_src):
        st = Stage(A3t, A3_BYTES, p)
        gB, t_gB = st.alloc([128, D], F32)
        p.dma("sp", gB, nrm[4, :].partition_broadcast(128), writes=[t_gB])
        xt = [st.alloc([128, D], F32) for _ in range(2)]
        yo = [st.alloc([128, D], F32) for _ in range(2)]
        junk, t_junk = st.alloc([128, D], BF16)
        ss = [st.alloc([128, 1], F32) for _ in range(2)]
        def ftile(i):
            b = i % 2
            x_, tx = xt[b]
            y_, ty = yo[b]
            s_, ts = ss[b]
            p.dma("sp", x_, h_src[i * 128:(i + 1) * 128, :], writes=[tx])
            p.op("dve", lambda e: e.memset(s_, 0.0), writes=[ts])
            yield
            p.op("act", lambda e: e.activation(junk, x_, AF.Square, accum_out=s_), reads=[tx], writes=[t_junk, ts])
            yield
            rstd_from_ss(s_, ts, D)
            yield
            p.op("dve", lambda e: e.scalar_tensor_tensor(y_, x_, s_, gB, ALU.mult, ALU.mult), reads=[tx, ts, t_gB], writes=[ty])
            yield
            for hf in range(2):
                p.dma("pool", y[(i - 1) * 128:i * 128, hf * 1024:(hf + 1) * 1024], y_[:, hf * 1024:(hf + 1) * 1024], reads=[ty])
        for i0 in range(1, NTI, 2):
            run_interleaved([ftile(i) for i in range(i0, min(NTI, i0 + 2))])

    def body():
        h_cur = h0
        for l in range(n_layers):
            tA1 = p.toks(5)
            tA2a, tA2m, tA2d = p.tok(), p.tok(), p.tok()
            with nc.named_scope('norm_stage'):
                norm_stage(h_cur, 2 * l, A1, tA1)
            tap("u%d" % l, A1t[:, :], [128, A_EL], tA1, BF16)
            if stop == "norm%d" % l:
                return
            with nc.named_scope('zproj_stage'):
                zproj_stage(l, tA1, tA2a)
            tap("prea%d" % l, A2t[:, 0:4 * NT], [128, 4 * NT], [tA2a], BF16)
            tap("sqT%d" % l, s_qT, [1024, NT], [], BF16)
            tap("sk%d" % l, s_k, [NT, 1024], [], BF16)
            tap("sgi%d" % l, s_gi, [4, NT], [])
            tap("sg%d" % l, s_g, [6144, NT], [], BF16)
            if stop == "zproj%d" % l:
                return
            with nc.named_scope('mlstm_stage'):
                mlstm_stage(l, tA2m)
            tap("hm%d" % l, A2t[:, 4 * NT:12 * NT], [128, 8 * NT], [tA2m], BF16)
            if stop == "mlstm%d" % l:
                return
            with nc.named_scope('attn_stage'):
                attn_stage(l, tA2d)
            tap("hd%d" % l, A2t[:, 12 * NT:16 * NT], [128, 4 * NT], [tA2d], BF16)
            if stop == "attn%d" % l:
                return
            tA1 = p.toks(5)
            with nc.named_scope('merge_stage'):
                merge_stage(l, [tA2a, tA2m, tA2d], tA1)
            tap("merged%d" % l, A1t[:, :], [128, A_EL], tA1, BF16)
            with nc.named_scope('wout_stage'):
                wout_stage(l, h_cur, hA, tA1)
            tap("hmix%d" % l, hA, [NT, D], [])
            if stop == "wout%d" % l:
                return
            tA1 = p.toks(5)
            with nc.named_scope('norm_stage'):
                norm_stage(hA, 2 * l + 1, A1, tA1)
            with nc.named_scope('ffn_up_stage'):
                wd = wd_views()
                ffn_up_stage(l, tA1, wd)
            with nc.named_scope('ffn_down_stage'):
                ffn_down_stage(l, hA, hB, wd)
            tap("hffn%d" % l, hB, [NT, D], [])
            if stop == "ffn%d" % l:
                return
            h_cur = hB
        final_stage(h_cur)
    body()
    p.finish()
    return nc, p


def _host_consts(inputs):
    cst = np.zeros((128, NCST), np.float32)
    cst[:, C_ID:C_ID + 128] = np.eye(128, dtype=np.float32)
    s = np.arange(128)[:, None]
    t = np.arange(128)[None, :]
    cst[:, C_TRI:C_TRI + 128] = (s <= t).astype(np.float32)
    cst[:, C_MV:C_MV + 128] = np.where(t < 112, NEG, 0.0)
    cst[:, C_MD:C_MD + 128] = np.where((s < 64) & (t >= 64), NEG, 0.0)
    ca = np.asarray(inputs["conv_a"], np.float32)
    cst[:, C_CA:C_CA + 24] = ca.reshape(2, 3, 4, 128).transpose(3, 0, 1, 2).reshape(128, 24)
    cf = np.asarray(inputs["conv_ffn"], np.float32)
    cst[:, C_CF:C_CF + 264] = cf.reshape(2, 3, 44, 128).transpose(3, 0, 1, 2).reshape(128, 264)
    fb = np.asarray(inputs["conv_ffn_b"], np.float32)
    cst[:, C_FB:C_FB + 88] = fb.reshape(2, 44, 128).transpose(2, 0, 1).reshape(128, 88)
    c68 = np.zeros((68, 260), np.float32)
    bif = np.asarray(inputs["b_if"], np.float32)
    for l in range(2):
        for w in range(2):
            c68[:, 2 * l + w] = np.repeat(bif[l, w], 17)
    c68[:, 132:260] = 1.0
    for h in range(4):
        c68[17 * h, 4:4 + 112] = -1.0e4
        c68[17 * h, 132:132 + 112] = 0.0
    return cst, c68


_CACHE = {}


def kernel(**inputs):
    x = np.asarray(inputs["x"], np.float32)
    meta = np.asarray(inputs["meta"], np.float32)
    B = x.shape[0]
    cst, c68 = _host_consts(inputs)
    nrm = np.ascontiguousarray(np.stack([inputs["norm_mix"][0], inputs["norm_ffn"][0], inputs["norm_mix"][1],
                                         inputs["norm_ffn"][1], inputs["norm_f"]]).astype(np.float32))
    w_br = np.ascontiguousarray(np.concatenate([inputs["w_br_a"], inputs["w_br_m"], inputs["w_br_d"]], axis=1).astype(np.float32))
    shared = {
        "w_in": np.ascontiguousarray(inputs["w_in"], np.float32),
        "w_br": w_br,
        "w_out": np.ascontiguousarray(inputs["w_out"], np.float32),
        "w_up": np.ascontiguousarray(inputs["w_up"], np.float32),
        "w_down": np.ascontiguousarray(inputs["w_down"], np.float32),
        "nrm": nrm, "cst": cst, "c68": c68,
        "mlg": np.ascontiguousarray(inputs["ml_norm"], np.float32),
        "dan": np.ascontiguousarray(inputs["da_norm"], np.float32),
        "dal": np.ascontiguousarray(np.asarray(inputs["da_lambda"], np.float32).reshape(2, 256)),
    }
    in_maps = []
    for b in range(B):
        h0 = np.zeros((NT, D), np.float32)
        h0[112:128] = meta
        h0[128:] = x[b]
        m = dict(shared)
        m["h0"] = h0
        in_maps.append(m)
    nc = build_program()[0]
    res = run_bass_kernel_spmd(nc, in_maps, core_ids=list(range(B)))
    return np.stack([np.asarray(r["y"], np.float32) for r in res.results], axis=0)
```

```python
import math, os
ZP = os.environ.get('ZP', 'a,fm,tm,if').split(',')
import numpy as np
import concourse.bass as bass
import concourse.mybir as mybir
from concourse.bass_utils import run_bass_kernel_spmd

F32 = mybir.dt.float32
BF16 = mybir.dt.bfloat16
AF = mybir.ActivationFunctionType
ALU = mybir.AluOpType
AX = mybir.AxisListType

NT = 2176
NTI = 17
D = 2048
KC = 16
D_IN = 13320
DFF = 5632
EPS = 1e-6
TGS = [(0, 512), (512, 512), (1024, 512), (1536, 512), (2048, 128)]
NEG = -30000.0

C_ID, C_TRI, C_MV, C_MD, C_CA, C_CF, C_FB, NCST = 0, 128, 256, 384, 512, 536, 800, 888


class Tok:
    __slots__ = ("w", "r", "x")

    def __init__(self, x=False):
        self.w = None
        self.r = {}
        self.x = x


class Prog:
    def __init__(self, nc, n_dma_sems=8):
        self.nc = nc
        self.eng = {"pe": nc.tensor, "act": nc.scalar, "dve": nc.vector,
                    "pool": nc.gpsimd, "sp": nc.sync}
        self.sems = {}
        self.cnt = {}
        for k in ("pe", "act", "dve", "pool"):
            self.sems[k] = nc.alloc_semaphore("s_" + k)
            self.cnt[k] = 0
        self.seen = {k: {} for k in self.eng}
        self.dring = {}
        for q in ("sp", "pool"):
            self.dring[q] = dict(
                sems=[nc.alloc_semaphore("d_%s%d" % (q, i)) for i in range(n_dma_sems)],
                cnt=[0] * n_dma_sems, nxt=0)
        self.ninstr = 0
        self.log = {k: [] for k in self.eng}

    def tok(self):
        return Tok()

    def toks(self, n):
        return [Tok() for _ in range(n)]

    def _semobj(self, key):
        if isinstance(key, str):
            return self.sems[key]
        q, i = key
        return self.dring[q]["sems"][i]

    def _collect(self, reads, writes):
        need = {}
        for t in reads:
            if t.w is not None:
                k, v = t.w
                if need.get(k, 0) < v:
                    need[k] = v
        for t in writes:
            if t.w is not None:
                k, v = t.w
                if need.get(k, 0) < v:
                    need[k] = v
            for k, v in t.r.items():
                if need.get(k, 0) < v:
                    need[k] = v
        return need

    def _emit_waits(self, e, need, attach=False):
        seen = self.seen[e]
        eng = self.eng[e]
        todo = []
        for k, v in need.items():
            if e == "pe" and k == "pe":
                continue
            if seen.get(k, 0) < v:
                todo.append((k, v))
                seen[k] = v
        held = todo.pop() if (attach and todo) else None
        for k, v in todo:
            eng.wait_ge(self._semobj(k), v)
            self.log[e].append(('w', k, v))
            self.ninstr += 1
        if held is not None:
            self.log[e].append(('w', held[0], held[1]))
        return held

    def _record(self, ev, reads, writes):
        k, v = ev
        for t in reads:
            if t.r.get(k, 0) < v:
                t.r[k] = v
        for t in writes:
            t.w = ev
            t.r = {}

    def op(self, e, fn, reads=(), writes=(), inc=True):
        xr = [t for t in reads if t.x]
        if xr:
            writes = list(writes) + xr
        need = self._collect(reads, writes)
        held = self._emit_waits(e, need, attach=True)
        ins = fn(self.eng[e])
        if held is not None:
            ins._wait_ge(self._semobj(held[0]), held[1])
        self.ninstr += 1
        if inc:
            self.cnt[e] += 1
            ins.then_inc(self.sems[e], 1)
            self.log[e].append(('i', e, 1))
            ev = (e, self.cnt[e])
        else:
            ev = (e, self.cnt[e] + 1)
        self._record(ev, reads, writes)
        return ins

    def dma(self, q, out, in_, reads=(), writes=()):
        ring = self.dring[q]
        i = ring["nxt"]
        ring["nxt"] = (i + 1) % len(ring["sems"])
        key = (q, i)
        need = self._collect(reads, writes)
        if ring["cnt"][i] > 0:
            need[key] = max(need.get(key, 0), 16 * ring["cnt"][i])
        held = self._emit_waits(q, need, attach=True)
        ins = self.eng[q].dma_start(out=out, in_=in_)
        if held is not None:
            ins._wait_ge(self._semobj(held[0]), held[1])
        self.ninstr += 1
        ring["cnt"][i] += 1
        ev = (key, 16 * ring["cnt"][i])
        ins.then_inc(ring["sems"][i], 16)
        self.log[q].append(('i', key, 16))
        self._record(ev, reads, writes)
        return ins

    def _all_events(self):
        need = {}
        for k in ("pe", "act", "dve", "pool"):
            if self.cnt[k] > 0:
                need[k] = self.cnt[k]
        for q, ring in self.dring.items():
            for i, c in enumerate(ring["cnt"]):
                if c > 0:
                    need[(q, i)] = 16 * c
        return need

    def barrier(self):
        need = self._all_events()
        for e in ("sp", "pool", "act", "dve", "pe"):
            n2 = {k: v for k, v in need.items() if k != e}
            self._emit_waits(e, n2)

    def finish(self):
        self._emit_waits("sp", self._all_events())


class Stage:
    def __init__(self, arena, nbytes, p):
        self.a = arena
        self.n = nbytes
        self.off = 0
        self.p = p

    def alloc(self, shape, dt):
        esz = 4 if dt == F32 else 2
        n = 1
        for s in shape[1:]:
            n *= s
        nb = (n * esz + 63) // 64 * 64
        assert self.off + nb <= self.n, ("stage arena overflow", self.off, nb, self.n)
        ap = self.a[:, self.off // 2: self.off // 2 + n * esz // 2]
        self.off += nb
        if dt == F32:
            ap = ap.bitcast(F32)
        if len(shape) == 3:
            ap = ap.rearrange("p (a b) -> p a b", b=shape[2])
        elif len(shape) == 4:
            ap = ap.rearrange("p (a b c) -> p a b c", b=shape[2], c=shape[3])
        return ap, self.p.tok()


def build_program(n_layers=2, taps=(), stop=None):
    nc = bass.Bass("TRN2", target_bir_lowering=False)
    p = Prog(nc)

    def din(name, shape):
        return nc.dram_tensor(name, shape, F32, kind="ExternalInput").ap()

    h0 = din("h0", [NT, D])
    w_in = din("w_in", [2, D, D_IN])
    w_br = din("w_br", [2, D, D])
    w_out = din("w_out", [2, D, D])
    w_up = din("w_up", [2, D, 2 * DFF])
    w_down = din("w_down", [2, DFF, D])
    nrm = din("nrm", [5, D])
    cst_d = din("cst", [128, NCST])
    c68_d = din("c68", [68, 260])
    mlg_d = din("mlg", [2, 1024])
    dan_d = din("dan", [2, 128])
    dal_d = din("dal", [2, 256])
    y = nc.dram_tensor("y", [2048, D], F32, kind="ExternalOutput").ap()
    tap_out = {}

    def scr(name, shape, dt):
        return nc.dram_tensor(name, shape, dt).ap()

    hA = scr("hA", [NT, D], F32)
    hB = scr("hB", [NT, D], F32)
    s_qT = scr("s_qT", [1024, NT], BF16)
    s_kT = scr("s_kT", [1024, NT], BF16)
    s_k = scr("s_k", [NT, 1024], BF16)
    s_v = scr("s_v", [NT, 1024], BF16)
    s_o = scr("s_o", [NT, 1024], BF16)
    s_gi = scr("s_gi", [4, NT], F32)
    s_gf = scr("s_gf", [4, NT], F32)
    s_dq = scr("s_dq", [512, NT], BF16)
    s_dk = scr("s_dk", [512, NT], BF16)
    s_dv = scr("s_dv", [NT, 1024], BF16)
    s_g = scr("s_g", [6144, NT], BF16)
    s_ff = scr("s_ff", [NTI, 128, 44, 128], BF16)

    cst = nc.alloc_sbuf_tensor("cst_sb", [128, NCST], F32)
    cstb = nc.alloc_sbuf_tensor("cstb", [128, 512], BF16)
    c68 = nc.alloc_sbuf_tensor("c68_sb", [128, 260], F32)
    A_EL = 16 * NT
    A1t = nc.alloc_sbuf_tensor("A1", [128, A_EL], BF16)
    A2t = nc.alloc_sbuf_tensor("A2", [128, A_EL], BF16)
    A3_BYTES = 66560
    A3t = nc.alloc_sbuf_tensor("A3", [128, A3_BYTES // 2], BF16)
    A1 = A1t[:, :].rearrange("p (k t) -> p k t", t=NT)
    A2 = A2t[:, :].rearrange("p (k t) -> p k t", t=NT)
    ps = [nc.alloc_psum_tensor("ps%d" % i, [128, 512], F32) for i in range(8)]
    tps = [Tok(x=True) for _ in range(8)]
    bank = [0]

    def nb():
        b = bank[0]
        bank[0] = (b + 1) % 8
        return b

    t_c = p.tok()
    p.dma("sp", cst[:, :], cst_d, writes=[t_c])
    p.dma("pool", cstb[:, :], cst_d[:, 0:512], writes=[t_c])
    p.dma("sp", c68[0:68, :], c68_d, writes=[t_c])
    epst = nc.alloc_sbuf_tensor("epst", [128, 1], F32)
    p.op("dve", lambda e: e.memset(epst[:, :], EPS), writes=[t_c])
    eps_c = epst[:, 0:1]
    ident_f = cst[:, C_ID:C_ID + 128]
    ident_b = cstb[:, 0:128]
    tri_b = cstb[:, 128:256]
    mv_b = cstb[:, 256:384]
    md_b = cstb[:, 384:512]

    def make_ring(banks):
        st_ = [0]

        def f():
            b = banks[st_[0] % len(banks)]
            st_[0] += 1
            return b
        return f

    def run_interleaved(gens):
        gens = list(gens)
        while gens:
            for g in list(gens):
                try:
                    next(g)
                except StopIteration:
                    gens.remove(g)

    rr = [0]

    def evac_copy(out, in_, reads, writes, scale=None):
        rr[0] ^= 1
        if rr[0]:
            if scale is None:
                p.op("act", lambda e: e.copy(out, in_), reads=reads, writes=writes)
            else:
                p.op("act", lambda e: e.mul(out, in_, scale), reads=reads, writes=writes)
        else:
            if scale is None:
                p.op("dve", lambda e: e.tensor_copy(out, in_), reads=reads, writes=writes)
            else:
                p.op("dve", lambda e: e.tensor_scalar(out, in_, scale, None, ALU.mult), reads=reads, writes=writes)

    def tap(name, src_ap, shape, tok_list, dt=F32):
        if name not in taps:
            return
        o = nc.dram_tensor("tap_" + name, shape, dt, kind="ExternalOutput").ap()
        tap_out[name] = o
        if len(shape) == 2 and shape[1] > 8192:
            n = shape[1]
            for c0 in range(0, n, 8192):
                c1 = min(n, c0 + 8192)
                p.dma("sp", o[:, c0:c1], src_ap[:, c0:c1], reads=tok_list)
        else:
            p.dma("sp", o, src_ap, reads=tok_list)

    def rstd_from_ss(ss, t_ss, n):
        p.op("act", lambda e: e.activation(ss, ss, AF.Ln, bias=eps_c, scale=1.0 / n), reads=[t_ss, t_c], writes=[t_ss])
        p.op("act", lambda e: e.activation(ss, ss, AF.Exp, scale=-0.5), reads=[t_ss], writes=[t_ss])

    def norm_stage(h_src, nrow, dst, t_dst):
        st = Stage(A3t, A3_BYTES, p)
        gB, t_gB = st.alloc([128, D], F32)
        p.dma("sp", gB, nrm[nrow, :].partition_broadcast(128), writes=[t_gB])
        xt = [st.alloc([128, D], F32) for _ in range(4)]
        xn = [st.alloc([128, D], BF16) for _ in range(4)]
        junk, t_junk = st.alloc([128, D], BF16)
        ss = [st.alloc([128, 1], F32) for _ in range(4)]
        nbn = make_ring([0, 1, 2, 3, 4, 5, 6, 7])

        def ntile(i):
            b = i % 4
            x_, tx = xt[b]
            n_, tn_ = xn[b]
            s_, ts = ss[b]
            p.dma("sp", x_, h_src[i * 128:(i + 1) * 128, :], writes=[tx])
            p.op("dve", lambda e: e.memset(s_, 0.0), writes=[ts])
            yield
            p.op("act", lambda e: e.activation(junk, x_, AF.Square, accum_out=s_), reads=[tx], writes=[t_junk, ts])
            yield
            rstd_from_ss(s_, ts, D)
            yield
            p.op("dve", lambda e: e.scalar_tensor_tensor(n_, x_, s_, gB, ALU.mult, ALU.mult),
                 reads=[tx, ts, t_gB], writes=[tn_])
            yield
            for g4 in range(4):
                bk = nbn()
                pbf = ps[bk][:, 0:256].bitcast(BF16)
                for j in range(4):
                    kc = g4 * 4 + j
                    p.op("pe", lambda e: e.transpose(pbf[:, j * 128:(j + 1) * 128], n_[:, kc * 128:(kc + 1) * 128], ident_b),
                         reads=[tn_, t_c], writes=[tps[bk]], inc=(j == 3))
                yield
                evac_copy(dst[:, g4 * 4:(g4 + 1) * 4, i * 128:(i + 1) * 128],
                          pbf.rearrange("p (a b) -> p a b", b=128), [tps[bk]], [t_dst[i // 4]])
                yield
        for i0 in range(0, NTI, 4):
            run_interleaved([ntile(i) for i in range(i0, min(NTI, i0 + 4))])
        p.barrier()

    def load_w(buf, tok, src2d, nkc, ncols, col_off=0):
        p.dma("pool", buf[:, 0:nkc, col_off:col_off + ncols],
              src2d.rearrange("(kc p) n -> p kc n", p=128), writes=[tok])

    def mm_fm(bk, wbuf, tw, c_lo, c_n, act, t_act, kcs, wk0, t0, tn):
        n = len(kcs)
        for i, kc in enumerate(kcs):
            p.op("pe", lambda e: e.matmul(ps[bk][0:c_n, 0:tn], wbuf[:, wk0 + i, c_lo:c_lo + c_n], act[:, kc, t0:t0 + tn],
                                          start=(i == 0), stop=(i == n - 1)),
                 reads=[tw] + t_act, writes=[tps[bk]], inc=(i == n - 1))

    def mm_tm(bk, act, t_act, tile, wbuf, tw, ncols, nkc):
        for kc in range(nkc):
            p.op("pe", lambda e: e.matmul(ps[bk][:, 0:ncols], act[:, kc, tile * 128:(tile + 1) * 128], wbuf[:, kc, 0:ncols],
                                          start=(kc == 0), stop=(kc == nkc - 1)),
                 reads=[tw] + t_act, writes=[tps[bk]], inc=(kc == nkc - 1))

    def zproj_stage(l, tA1, tA2a):
        st = Stage(A3t, A3_BYTES, p)
        wb = [st.alloc([128, 16, 512], BF16) for _ in range(2)]
        ob = [st.alloc([128, NT], BF16) for _ in range(2)]
        tb = [st.alloc([128, 512], BF16) for _ in range(3)]
        trow, t_trow = st.alloc([128, NT + 2], F32)
        axs = [st.alloc([128, 512], F32) for _ in range(2)]
        acc = [st.alloc([128, 512], F32) for _ in range(2)]
        gtmp, t_gtmp = st.alloc([128, 512], F32)
        W = w_in[l]
        wi = [0]
        oi = [0]
        ti = [0]

        def next_wb():
            wi[0] ^= 1
            return wb[wi[0]]

        p.op("dve", lambda e: e.memset(trow[:, 0:2], 0.0), writes=[t_trow])
        q = 0
        for c in (range(4) if 'a' in ZP else []):
            w_, tw = next_wb()
            for s, base in enumerate((0, 512, 1024)):
                load_w(w_, tw, W[:, base + c * 128: base + (c + 1) * 128], 16, 128, col_off=s * 128)
            for (t0, tn) in TGS:
                bx, bc, bb = nb(), nb(), nb()
                tg = [tA1[t0 // 512]]
                mm_fm(bx, w_, tw, 0, 128, A1, tg, range(16), 0, t0, tn)
                mm_fm(bc, w_, tw, 256, 128, A1, tg, range(16), 0, t0, tn)
                mm_fm(bb, w_, tw, 128, 128, A1, tg, range(16), 0, t0, tn)
                q ^= 1
                ax_, tax = axs[q]
                ac_, tac = acc[q]
                p.op("act", lambda e: e.copy(ax_[:, 0:tn], ps[bx][:, 0:tn]), reads=[tps[bx]], writes=[tax])
                p.op("dve", lambda e: e.tensor_tensor(trow[:, 2 + t0:2 + t0 + tn], ps[bc][:, 0:tn], ax_[:, 0:tn], ALU.mult),
                     reads=[tps[bc], tax], writes=[t_trow])

                def cw(j):
                    k = C_CA + (l * 3 + j) * 4 + c
                    return cst[:, k:k + 1]
                p.op("dve", lambda e: e.tensor_scalar(ac_[:, 0:tn], trow[:, t0:t0 + tn], cw(0), None, ALU.mult),
                     reads=[t_trow, t_c], writes=[tac])
                p.op("dve", lambda e: e.scalar_tensor_tensor(ac_[:, 0:tn], trow[:, t0 + 1:t0 + 1 + tn], cw(1), ac_[:, 0:tn], ALU.mult, ALU.add),
                     reads=[t_trow, tac], writes=[tac])
                p.op("dve", lambda e: e.scalar_tensor_tensor(ac_[:, 0:tn], trow[:, t0 + 2:t0 + 2 + tn], cw(2), ac_[:, 0:tn], ALU.mult, ALU.add),
                     reads=[t_trow, tac], writes=[tac])
                p.op("dve", lambda e: e.tensor_tensor(A2[:, c, t0:t0 + tn], ac_[:, 0:tn], ps[bb][:, 0:tn], ALU.mult),
                     reads=[tac, tps[bb]], writes=[tA2a])
        fm_segs = [(1536, 8, s_qT, None, None), (2560, 8, s_kT, 1.0 / 16, None),
                   (5640, 4, s_dq, 1.0 / 8, None), (6152, 4, s_dk, None, None),
                   (7176, 48, s_g, None, AF.Sigmoid)]
        for (c0, nch, dst, scale, func) in (fm_segs if 'fm' in ZP else []):
            for g in range(nch // 4):
                w_, tw = next_wb()
                load_w(w_, tw, W[:, c0 + g * 512: c0 + (g + 1) * 512], 16, 512)
                for ch in range(4):
                    oi[0] ^= 1
                    o_, to = ob[oi[0]]
                    for (t0, tn) in TGS:
                        bk = nb()
                        mm_fm(bk, w_, tw, ch * 128, 128, A1, [tA1[t0 // 512]], range(16), 0, t0, tn)
                        if func is not None:
                            p.op("act", lambda e: e.activation(o_[:, t0:t0 + tn], ps[bk][:, 0:tn], func),
                                 reads=[tps[bk]], writes=[to])
                        else:
                            evac_copy(o_[:, t0:t0 + tn], ps[bk][:, 0:tn], [tps[bk]], [to], scale=scale)
                    r0 = (g * 4 + ch) * 128
                    p.dma("sp", dst[r0:r0 + 128, :], o_, reads=[to])
        tm_segs = [(2560, 1024, s_k, 1.0 / 16, None), (3584, 1024, s_v, None, None),
                   (4608, 1024, s_o, None, AF.Sigmoid), (6664, 512, s_dv, None, None)]
        if 'tm1' in ZP:
            tm_segs = tm_segs[3:4]
        for (c0, ncol, dst, scale, func) in (tm_segs if ('tm' in ZP or 'tm1' in ZP) else []):
            for g in range(ncol // 512):
                w_, tw = next_wb()
                load_w(w_, tw, W[:, c0 + g * 512: c0 + (g + 1) * 512], 16, 512)
                for i in range(NTI):
                    bk = nb()
                    mm_tm(bk, A1, [tA1[i // 4]], i, w_, tw, 512, 16)
                    ti[0] = (ti[0] + 1) % 3
                    t_, tt = tb[ti[0]]
                    if func is not None:
                        p.op("act", lambda e: e.activation(t_, ps[bk][:, :], func), reads=[tps[bk]], writes=[tt])
                    else:
                        evac_copy(t_, ps[bk][:, :], [tps[bk]], [tt], scale=scale)
                    p.dma("sp", dst[i * 128:(i + 1) * 128, g * 512:(g + 1) * 512], t_, reads=[tt])
        w_, tw = next_wb()
        load_w(w_, tw, W[:, 5632:5640], 16, 8)
        for which, dst in (((0, s_gi), (1, s_gf)) if 'if' in ZP else []):
            for (t0, tn) in TGS:
                bk = nb()
                mm_fm(bk, w_, tw, which * 4, 4, A1, [tA1[t0 // 512]], range(16), 0, t0, tn)
                p.op("dve", lambda e: e.tensor_copy(gtmp[0:4, 0:tn], ps[bk][0:4, 0:tn]), reads=[tps[bk]], writes=[t_gtmp])
                p.dma("sp", dst[:, t0:t0 + tn], gtmp[0:4, 0:tn], reads=[t_gtmp])
        p.barrier()

    def mlstm_stage(l, tA2m):
        st = Stage(A3t, A3_BYTES, p)

        def g68():
            return st.alloc([128, 128], F32)
        gi, t_gi = g68()
        gf, t_gf = g68()
        lf, t_lf = g68()
        bb, t_bb = g68()
        aa, t_aa = g68()
        wk, t_wk = g68()
        ee, t_ee = g68()
        onesm, t_on = g68()
        amax, t_amax = st.alloc([128, 1], F32)
        negRc, t_negRc = st.alloc([128, 1], F32)
        rows = [st.alloc([128, 68], F32) for _ in range(7)]
        (rowA, t_rA), (rowB, t_rB), (mnext, t_mn), (mprev, t_mp), (Rrow, t_R), (negR, t_nR), (gsrow, t_gs) = rows
        wkcol, t_wkc = st.alloc([128, 68], F32)
        ecol, t_ec = st.alloc([128, 68], F32)
        gsB, t_gsB = st.alloc([128, 68], F32)
        mlgB, t_mlg = st.alloc([128, 1024], F32)
        C, t_C = st.alloc([128, 8, 257], F32)
        Cd, t_Cd = st.alloc([128, 8, 257], F32)
        Cb, t_Cb = st.alloc([128, 8, 257], BF16)
        qT = [st.alloc([128, 8, 128], BF16) for _ in range(2)]
        kT = [st.alloc([128, 8, 128], BF16) for _ in range(2)]
        kt = [st.alloc([128, 1024], BF16) for _ in range(2)]
        va = [st.alloc([128, 4, 257], BF16) for _ in range(2)]
        ot = [st.alloc([128, 1024], BF16) for _ in range(2)]
        Pb = [st.alloc([128, 128], BF16) for _ in range(4)]
        vw = [st.alloc([128, 257], BF16) for _ in range(4)]
        hn = [st.alloc([128, 256], F32) for _ in range(4)]
        t_Ch, t_Cdh, t_Cbh = p.toks(4), p.toks(4), p.toks(4)
        hm = [st.alloc([128, 1024], BF16) for _ in range(2)]
        junk, t_junk = st.alloc([128, 256], BF16)
        sm = [st.alloc([128, 4], F32) for _ in range(4)]

        p.dma("sp", gi[0:68, :], s_gi.rearrange("h (c t) -> (h c) t", t=128), writes=[t_gi])
        p.dma("sp", gf[0:68, :], s_gf.rearrange("h (c t) -> (h c) t", t=128), writes=[t_gf])
        p.dma("sp", mlgB, mlg_d[l, :].partition_broadcast(128), writes=[t_mlg])
        bi = c68[0:68, 2 * l:2 * l + 1]
        bf_ = c68[0:68, 2 * l + 1:2 * l + 2]
        padneg = c68[0:68, 4:132]
        valid = c68[0:68, 132:260]
        S = slice(0, 68)
        p.op("dve", lambda e: e.tensor_scalar(aa[S, :], gi[S, :], bi, None, ALU.add), reads=[t_gi, t_c], writes=[t_aa])
        p.op("act", lambda e: e.activation(lf[S, :], gf[S, :], AF.Sigmoid, bias=bf_), reads=[t_gf, t_c], writes=[t_lf])
        p.op("act", lambda e: e.activation(lf[S, :], lf[S, :], AF.Ln), reads=[t_lf], writes=[t_lf])
        p.op("dve", lambda e: e.tensor_tensor(lf[S, :], lf[S, :], valid, ALU.mult), reads=[t_lf, t_c], writes=[t_lf])
        p.op("dve", lambda e: e.memset(onesm[:, :], 1.0), writes=[t_on])
        p.op("dve", lambda e: e.tensor_tensor_scan(bb[S, :], onesm[S, :], lf[S, :], 0.0, ALU.mult, ALU.add),
             reads=[t_on, t_lf], writes=[t_bb])
        p.op("dve", lambda e: e.tensor_tensor(aa[S, :], aa[S, :], bb[S, :], ALU.subtract), reads=[t_aa, t_bb], writes=[t_aa])
        p.op("dve", lambda e: e.tensor_tensor(aa[S, :], aa[S, :], padneg, ALU.add), reads=[t_aa, t_c], writes=[t_aa])
        p.op("dve", lambda e: e.reduce_max(amax[S, :], aa[S, :], AX.X), reads=[t_aa], writes=[t_amax])
        bk = nb()
        p.op("pe", lambda e: e.matmul(ps[bk][0:1, 0:68], amax[S, 0:1], ident_f[0:68, 0:68], start=True, stop=True),
             reads=[t_amax, t_c], writes=[tps[bk]])
        p.op("dve", lambda e: e.tensor_copy(rowA[0:1, :], ps[bk][0:1, 0:68]), reads=[tps[bk]], writes=[t_rA])
        bk = nb()
        p.op("pe", lambda e: e.matmul(ps[bk][0:1, 0:68], bb[S, 127:128], ident_f[0:68, 0:68], start=True, stop=True),
             reads=[t_bb, t_c], writes=[tps[bk]])
        p.op("dve", lambda e: e.tensor_copy(rowB[0:1, :], ps[bk][0:1, 0:68]), reads=[tps[bk]], writes=[t_rB])
        for h in range(4):
            sl = slice(h * 17, (h + 1) * 17)
            p.op("dve", lambda e: e.tensor_tensor_scan(mnext[0:1, sl], rowA[0:1, sl], rowB[0:1, sl], 0.0, ALU.max, ALU.add),
                 reads=[t_rA, t_rB], writes=[t_mn])
        p.op("dve", lambda e: e.memset(mprev[0:1, :], 0.0), writes=[t_mp])
        for h in range(4):
            p.op("dve", lambda e: e.tensor_copy(mprev[0:1, h * 17 + 1:h * 17 + 17], mnext[0:1, h * 17:h * 17 + 16]),
                 reads=[t_mn], writes=[t_mp])
        p.op("dve", lambda e: e.tensor_tensor(Rrow[0:1, :], mprev[0:1, :], rowA[0:1, :], ALU.max), reads=[t_mp, t_rA], writes=[t_R])
        p.op("dve", lambda e: e.tensor_scalar(negR[0:1, :], Rrow[0:1, :], -1.0, None, ALU.mult), reads=[t_R], writes=[t_nR])
        p.op("dve", lambda e: e.tensor_tensor(gsrow[0:1, :], mprev[0:1, :], Rrow[0:1, :], ALU.subtract), reads=[t_mp, t_R], writes=[t_gs])
        p.op("act", lambda e: e.activation(gsrow[0:1, :], gsrow[0:1, :], AF.Exp), reads=[t_gs], writes=[t_gs])
        bk = nb()
        p.op("pe", lambda e: e.matmul(ps[bk][0:68, 0:1], negR[0:1, :], ident_f[0:1, 0:1], start=True, stop=True),
             reads=[t_nR, t_c], writes=[tps[bk]])
        p.op("dve", lambda e: e.tensor_copy(negRc[S, :], ps[bk][0:68, 0:1]), reads=[tps[bk]], writes=[t_negRc])
        p.op("act", lambda e: e.activation(wk[S, :], aa[S, :], AF.Exp, bias=negRc[S, 0:1]), reads=[t_aa, t_negRc], writes=[t_wk])
        p.op("act", lambda e: e.activation(ee[S, :], bb[S, :], AF.Exp, bias=negRc[S, 0:1], scale=-1.0), reads=[t_bb, t_negRc], writes=[t_ee])
        for src, tsrc, dst, tdst in ((wk, t_wk, wkcol, t_wkc), (ee, t_ee, ecol, t_ec)):
            bk = nb()
            p.op("pe", lambda e: e.matmul(ps[bk][:, 0:68], src[S, :], ident_f[0:68, 0:68], start=True, stop=True),
                 reads=[tsrc, t_c], writes=[tps[bk]])
            p.op("dve", lambda e: e.tensor_copy(dst[:, :], ps[bk][:, 0:68]), reads=[tps[bk]], writes=[tdst])
        bk = nb()
        p.op("pe", lambda e: e.matmul(ps[bk][:, 0:68], onesm[0:1, :], gsrow[0:1, :], start=True, stop=True),
             reads=[t_on, t_gs], writes=[tps[bk]])
        p.op("dve", lambda e: e.tensor_copy(gsB[:, :], ps[bk][:, 0:68]), reads=[tps[bk]], writes=[t_gsB])
        tap("wkcol%d" % l, wkcol, [128, 68], [t_wkc])
        tap("ecol%d" % l, ecol, [128, 68], [t_ec])
        tap("gsB%d" % l, gsB, [128, 68], [t_gsB])

        p.op("dve", lambda e: e.memset(C[:, :, :], 0.0), writes=t_Ch)
        for r in range(2):
            p.op("dve", lambda e: e.memset(va[r][0][:, :, 256:257], 1.0), writes=[va[r][1]])
        nbs = make_ring([4, 5, 6, 7])
        qTs = s_qT.rearrange("(c p) t -> p c t", p=128)
        kTs = s_kT.rearrange("(c p) t -> p c t", p=128)
        for i in range(NTI):
            r = i % 2
            cols = slice(i * 128, (i + 1) * 128)
            q_, tq = qT[r]
            kT_, tkT = kT[r]
            k_, tk = kt[r]
            v_, tv = va[r]
            o_, to = ot[r]
            hm_, thm = hm[r]
            p.dma("sp", q_, qTs[:, :, cols], writes=[tq])
            p.dma("sp", kT_, kTs[:, :, cols], writes=[tkT])
            p.dma("sp", k_, s_k[cols, :], writes=[tk])
            p.dma("sp", v_[:, :, 0:256], s_v[cols, :].rearrange("p (h e) -> p h e", e=256), writes=[tv])
            p.dma("sp", o_, s_o[cols, :], writes=[to])
            def unit(h):
                idx = 17 * h + i
                P_, tP = Pb[h]
                vw_, tvw = vw[h]
                hn_, thn = hn[h]
                sm_, tsm = sm[h]
                hs = slice(2 * h, 2 * h + 2)
                bs = nbs()
                for half in range(2):
                    p.op("pe", lambda e: e.matmul(ps[bs][:, 0:128], kT_[:, 2 * h + half, :], q_[:, 2 * h + half, :],
                                                  start=(half == 0), stop=(half == 1)),
                         reads=[tkT, tq], writes=[tps[bs]], inc=(half == 1))
                yield
                p.op("dve", lambda e: e.tensor_tensor(P_, ps[bs][:, 0:128], tri_b, ALU.mult), reads=[tps[bs], t_c], writes=[tP])
                yield
                p.op("dve", lambda e: e.tensor_scalar(Cd[:, hs, :], C[:, hs, :], gsB[:, idx:idx + 1], None, ALU.mult),
                     reads=[t_Ch[h], t_gsB], writes=[t_Cdh[h]])
                yield
                p.op("act", lambda e: e.copy(Cb[:, hs, :], Cd[:, hs, :]), reads=[t_Cdh[h]], writes=[t_Cbh[h]])
                yield
                p.op("act", lambda e: e.activation(vw_, v_[:, h, :], AF.Copy, scale=wkcol[:, idx:idx + 1]),
                     reads=[tv, t_wkc], writes=[tvw])
                yield
                bn = h
                for half in range(2):
                    p.op("pe", lambda e: e.matmul(ps[bn][:, 0:257], q_[:, 2 * h + half, :], Cb[:, 2 * h + half, :],
                                                  start=(half == 0), stop=False),
                         reads=[tq, t_Cbh[h]], writes=[tps[bn]], inc=False)
                p.op("pe", lambda e: e.matmul(ps[bn][:, 0:257], P_, vw_, start=False, stop=True),
                     reads=[tP, tvw], writes=[tps[bn]])
                yield
                for half in range(2):
                    bc = nbs()
                    p.op("pe", lambda e: e.matmul(ps[bc][:, 0:257], k_[:, h * 256 + half * 128:h * 256 + (half + 1) * 128], vw_,
                                                  start=True, stop=True),
                         reads=[tk, tvw], writes=[tps[bc]])
                    yield
                    p.op("dve", lambda e: e.tensor_tensor(C[:, 2 * h + half, :], Cd[:, 2 * h + half, :], ps[bc][:, 0:257], ALU.add),
                         reads=[t_Cdh[h], tps[bc]], writes=[t_Ch[h]])
                    yield
                dn = sm_[:, 0:1]
                rc = sm_[:, 1:2]
                ssq = sm_[:, 2:3]
                r2 = sm_[:, 3:4]
                p.op("dve", lambda e: e.tensor_copy(dn, ps[bn][:, 256:257]), reads=[tps[bn]], writes=[tsm])
                p.op("dve", lambda e: e.scalar_tensor_tensor(dn, dn, -1.0, dn, ALU.mult, ALU.max), reads=[tsm], writes=[tsm])
                p.op("dve", lambda e: e.tensor_tensor(dn, dn, ecol[:, idx:idx + 1], ALU.max), reads=[tsm, t_ec], writes=[tsm])
                yield
                p.op("dve", lambda e: e.reciprocal(rc, dn), reads=[tsm], writes=[tsm])
                p.op("dve", lambda e: e.memset(ssq, 0.0), reads=[tsm], writes=[tsm])
                yield
                p.op("act", lambda e: e.activation(junk, ps[bn][:, 0:256], AF.Square, scale=rc, accum_out=ssq),
                     reads=[tps[bn], tsm], writes=[t_junk, tsm])
                yield
                rstd_from_ss(ssq, tsm, 256)
                yield
                p.op("dve", lambda e: e.tensor_tensor(r2, ssq, rc, ALU.mult), reads=[tsm], writes=[tsm])
                yield
                p.op("dve", lambda e: e.scalar_tensor_tensor(hn_, ps[bn][:, 0:256], r2, mlgB[:, h * 256:(h + 1) * 256], ALU.mult, ALU.mult),
                     reads=[tps[bn], tsm, t_mlg], writes=[thn])
                yield
                p.op("dve", lambda e: e.tensor_tensor(hm_[:, h * 256:(h + 1) * 256], hn_, o_[:, h * 256:(h + 1) * 256], ALU.mult),
                     reads=[thn, to], writes=[thm])
            run_interleaved([unit(h) for h in range(4)])
            for g2 in range(2):
                bk = nb()
                pbf = ps[bk][:, 0:256].bitcast(BF16)
                for j in range(4):
                    cc = g2 * 4 + j
                    p.op("pe", lambda e: e.transpose(pbf[:, j * 128:(j + 1) * 128], hm_[:, cc * 128:(cc + 1) * 128], ident_b),
                         reads=[thm, t_c], writes=[tps[bk]], inc=(j == 3))
                evac_copy(A2[:, 4 + g2 * 4:4 + (g2 + 1) * 4, cols], pbf.rearrange("p (a b) -> p a b", b=128), [tps[bk]], [tA2m])
        p.barrier()

    def attn_stage(l, tA2d):
        st = Stage(A3t, A3_BYTES, p)
        dq = A1[:, 0:4, :]
        dk = A1[:, 4:8, :]
        dv = A1t[:, 8 * NT:8 * NT + NTI * 512].rearrange("p (i e) -> p i e", e=512)
        t_qkv = p.tok()
        p.dma("sp", dq, s_dq.rearrange("(h p) t -> p h t", p=128), writes=[t_qkv])
        p.dma("sp", dk, s_dk.rearrange("(h p) t -> p h t", p=128), writes=[t_qkv])
        p.dma("sp", dv, s_dv[:, 0:512].rearrange("(i p) e -> p i e", p=128), writes=[t_qkv])
        sc = [st.alloc([128, NT], F32) for _ in range(2)]
        PmA = A1t[:, 8 * NT + NTI * 512:16 * NT].rearrange("p (a t) -> p a t", t=NT)
        Pm = [[(PmA[:, 2 * s_ + m_, :], p.tok()) for m_ in range(2)] for s_ in range(2)]
        Aa = [st.alloc([128, NT], BF16) for _ in range(2)]
        tmpA = [st.alloc([128, NT], BF16) for _ in range(2)]
        AT = [st.alloc([128, NTI, 128], BF16) for _ in range(2)]
        hd = [st.alloc([128, 512], BF16) for _ in range(2)]
        dalB, t_dal = st.alloc([128, 256], F32)
        danS, t_dan = st.alloc([128, 128], F32)
        junk, t_junk = st.alloc([128, 128], BF16)
        lam, t_lam = st.alloc([128, 4], F32)
        mx = [[st.alloc([128, 8], F32) for _ in range(2)] for _ in range(2)]
        sm = [st.alloc([128, 8], F32) for _ in range(2)]
        lam_init = 0.8 - 0.6 * math.exp(-0.3 * l)
        p.dma("sp", dalB, dal_d[l, :].partition_broadcast(128), writes=[t_dal])
        p.dma("sp", danS, dan_d[l, :].partition_broadcast(128), writes=[t_dan])
        p.op("dve", lambda e: e.tensor_scalar(danS, danS, 1.0 - lam_init, None, ALU.mult), reads=[t_dan], writes=[t_dan])
        p.op("dve", lambda e: e.tensor_tensor(dalB[:, 0:64], dalB[:, 0:64], dalB[:, 64:128], ALU.mult), reads=[t_dal], writes=[t_dal])
        p.op("dve", lambda e: e.tensor_tensor(dalB[:, 128:192], dalB[:, 128:192], dalB[:, 192:256], ALU.mult), reads=[t_dal], writes=[t_dal])
        p.op("dve", lambda e: e.reduce_sum(lam[:, 0:1], dalB[:, 0:64], AX.X), reads=[t_dal], writes=[t_lam])
        p.op("dve", lambda e: e.reduce_sum(lam[:, 1:2], dalB[:, 128:192], AX.X), reads=[t_dal], writes=[t_lam])
        p.op("act", lambda e: e.activation(lam[:, 0:2], lam[:, 0:2], AF.Exp), reads=[t_lam], writes=[t_lam])
        p.op("dve", lambda e: e.tensor_tensor(lam[:, 2:3], lam[:, 0:1], lam[:, 1:2], ALU.subtract), reads=[t_lam], writes=[t_lam])
        p.op("dve", lambda e: e.tensor_scalar(lam[:, 2:3], lam[:, 2:3], lam_init, -1.0, ALU.add, ALU.mult), reads=[t_lam], writes=[t_lam])
        nlam = lam[:, 2:3]
        nba = make_ring([2, 3, 4, 5, 6, 7])
        for j in range(NTI):
            kend = (j + 1) * 128
            qcols = slice(j * 128, (j + 1) * 128)
            hd_, thd = hd[j % 2]

            def aunit(h, sl):
                sm_, tsm = sm[sl]
                AT_, tAT = AT[sl]
                sc_, tsc = sc[sl]
                A_, tA = Aa[sl]
                tm_, ttm = tmpA[sl]
                for m in range(2):
                    Pm_, tPm = Pm[sl][m]
                    mx_, tmx = mx[sl][m]
                    rs = slice(m * 64, (m + 1) * 64)
                    nblk5 = (kend + 511) // 512
                    for b5 in range(nblk5):
                        k0 = b5 * 512
                        kn = min(512, kend - k0)
                        masks = []
                        if k0 == 0:
                            masks.append((0, mv_b))
                        if j >= 1 and k0 <= j * 128 < k0 + kn:
                            masks.append((j * 128 - k0, md_b))
                        bk = nba()
                        p.op("pe", lambda e: e.matmul(ps[bk][:, 0:kn], dq[rs, h, qcols], dk[rs, h, k0:k0 + kn], start=True, stop=(len(masks) == 0)),
                             reads=[t_qkv], writes=[tps[bk]], inc=(len(masks) == 0))
                        for mi, (co, mk) in enumerate(masks):
                            p.op("pe", lambda e: e.matmul(ps[bk][:, co:co + 128], ident_b, mk, start=False, stop=(mi == len(masks) - 1)),
                                 reads=[t_c], writes=[tps[bk]], inc=(mi == len(masks) - 1))
                        yield
                        p.op("dve", lambda e: e.reduce_max(mx_[:, b5:b5 + 1], ps[bk][:, 0:kn], AX.X), reads=[tps[bk]], writes=[tmx])
                        yield
                        p.op("act", lambda e: e.copy(sc_[:, k0:k0 + kn], ps[bk][:, 0:kn]), reads=[tps[bk]], writes=[tsc])
                        yield
                    p.op("dve", lambda e: e.reduce_max(mx_[:, 6:7], mx_[:, 0:nblk5], AX.X), reads=[tmx], writes=[tmx])
                    p.op("dve", lambda e: e.tensor_scalar(mx_[:, 7:8], mx_[:, 6:7], -1.0, None, ALU.mult), reads=[tmx], writes=[tmx])
                    p.op("dve", lambda e: e.memset(sm_[:, m:m + 1], 0.0), writes=[tsm])
                    yield
                    p.op("act", lambda e: e.activation(Pm_[:, 0:kend], sc_[:, 0:kend], AF.Exp, bias=mx_[:, 7:8], accum_out=sm_[:, m:m + 1]),
                         reads=[tsc, tmx, tsm], writes=[tPm, tsm])
                    yield
                p.op("dve", lambda e: e.reciprocal(sm_[:, 2:4], sm_[:, 0:2]), reads=[tsm], writes=[tsm])
                p.op("dve", lambda e: e.tensor_tensor(sm_[:, 4:5], sm_[:, 3:4], nlam, ALU.mult), reads=[tsm, t_lam], writes=[tsm])
                yield
                p.op("act", lambda e: e.activation(tm_[:, 0:kend], Pm[sl][0][0][:, 0:kend], AF.Copy, scale=sm_[:, 2:3]),
                     reads=[Pm[sl][0][1], tsm], writes=[ttm])
                yield
                p.op("dve", lambda e: e.scalar_tensor_tensor(A_[:, 0:kend], Pm[sl][1][0][:, 0:kend], sm_[:, 4:5], tm_[:, 0:kend], ALU.mult, ALU.add),
                     reads=[Pm[sl][1][1], tsm, ttm], writes=[tA])
                yield
                nk = j + 1
                for g4 in range((nk + 3) // 4):
                    bk = nba()
                    pbf = ps[bk][:, 0:256].bitcast(BF16)
                    n4 = min(4, nk - g4 * 4)
                    for jj in range(n4):
                        kb = g4 * 4 + jj
                        p.op("pe", lambda e: e.transpose(pbf[:, jj * 128:(jj + 1) * 128], A_[:, kb * 128:(kb + 1) * 128], ident_b),
                             reads=[tA, t_c], writes=[tps[bk]], inc=(jj == n4 - 1))
                    yield
                    evac_copy(AT_[:, g4 * 4:g4 * 4 + n4, :], pbf[:, 0:n4 * 128].rearrange("p (a b) -> p a b", b=128), [tps[bk]], [tAT])
                    yield
                bo = sl
                for kb in range(nk):
                    p.op("pe", lambda e: e.matmul(ps[bo][:, 0:128], AT_[:, kb, :], dv[:, kb, h * 128:(h + 1) * 128],
                                                  start=(kb == 0), stop=(kb == nk - 1)),
                         reads=[tAT, t_qkv], writes=[tps[bo]], inc=(kb == nk - 1))
                yield
                ssq = sm_[:, 5:6]
                p.op("dve", lambda e: e.memset(ssq, 0.0), reads=[tsm], writes=[tsm])
                yield
                p.op("act", lambda e: e.activation(junk, ps[bo][:, 0:128], AF.Square, accum_out=ssq), reads=[tps[bo], tsm], writes=[t_junk, tsm])
                yield
                rstd_from_ss(ssq, tsm, 128)
                yield
                p.op("dve", lambda e: e.scalar_tensor_tensor(hd_[:, h * 128:(h + 1) * 128], ps[bo][:, 0:128], ssq, danS, ALU.mult, ALU.mult),
                     reads=[tps[bo], tsm, t_dan], writes=[thd])
            run_interleaved([aunit(0, 0), aunit(1, 1)])
            run_interleaved([aunit(2, 0), aunit(3, 1)])
            if j == 0:
                p.op("dve", lambda e: e.memset(hd_[0:112, :], 0.0), writes=[thd])
            bk = nb()
            pbf = ps[bk][:, 0:256].bitcast(BF16)
            for h in range(4):
                p.op("pe", lambda e: e.transpose(pbf[:, h * 128:(h + 1) * 128], hd_[:, h * 128:(h + 1) * 128], ident_b),
                     reads=[thd, t_c], writes=[tps[bk]], inc=(h == 3))
            evac_copy(A2[:, 12:16, qcols], pbf.rearrange("p (a b) -> p a b", b=128), [tps[bk]], [tA2d])
        p.barrier()

    def merge_stage(l, tA2, tA1):
        st = Stage(A3t, A3_BYTES, p)
        wb = [st.alloc([128, 16, 512], BF16) for _ in range(2)]
        sg = [st.alloc([128, 3, NT], BF16) for _ in range(2)]
        m1, t_m1 = st.alloc([128, 512], F32)
        m2, t_m2 = st.alloc([128, 512], F32)
        m3, t_m3 = st.alloc([128, 512], F32)
        sgs = s_g.rearrange("(b j p) t -> p b j t", b=3, p=128)
        load_w(wb[0][0], wb[0][1], w_br[l][:, 0:512], 16, 512)
        for g in range(4):
            w_, tw = wb[g % 2]
            if g + 1 < 4:
                load_w(wb[(g + 1) % 2][0], wb[(g + 1) % 2][1], w_br[l][:, (g + 1) * 512:(g + 2) * 512], 16, 512)
            for jj in range(4):
                j = g * 4 + jj
                s_, ts = sg[j % 2]
                p.dma("sp", s_, sgs[:, :, j, :], writes=[ts])
                for (t0, tn) in TGS:
                    ba, bm, bd = nb(), nb(), nb()
                    mm_fm(ba, w_, tw, jj * 128, 128, A2, tA2, range(0, 4), 0, t0, tn)
                    mm_fm(bm, w_, tw, jj * 128, 128, A2, tA2, range(4, 12), 4, t0, tn)
                    mm_fm(bd, w_, tw, jj * 128, 128, A2, tA2, range(12, 16), 12, t0, tn)
                    p.op("dve", lambda e: e.tensor_tensor(m1[:, 0:tn], ps[ba][:, 0:tn], s_[:, 0, t0:t0 + tn], ALU.mult),
                         reads=[tps[ba], ts], writes=[t_m1])
                    p.op("dve", lambda e: e.tensor_tensor(m2[:, 0:tn], ps[bm][:, 0:tn], s_[:, 1, t0:t0 + tn], ALU.mult),
                         reads=[tps[bm], ts], writes=[t_m2])
                    p.op("dve", lambda e: e.tensor_tensor(m3[:, 0:tn], ps[bd][:, 0:tn], s_[:, 2, t0:t0 + tn], ALU.mult),
                         reads=[tps[bd], ts], writes=[t_m3])
                    p.op("dve", lambda e: e.tensor_tensor(m1[:, 0:tn], m1[:, 0:tn], m2[:, 0:tn], ALU.add),
                         reads=[t_m1, t_m2], writes=[t_m1])
                    p.op("dve", lambda e: e.tensor_tensor(A1[:, j, t0:t0 + tn], m1[:, 0:tn], m3[:, 0:tn], ALU.add),
                         reads=[t_m1, t_m3], writes=[tA1[t0 // 512]])
        p.barrier()

    def wout_stage(l, h_src, h_dst, tA1):
        st = Stage(A3t, A3_BYTES, p)
        wb = [st.alloc([128, 16, 512], BF16) for _ in range(2)]
        hx = [st.alloc([128, 512], F32) for _ in range(4)]
        hn = [st.alloc([128, 512], F32) for _ in range(4)]
        items = [(cg, i) for cg in range(4) for i in range(NTI)]

        def issue_load(n):
            cg, i = items[n]
            x_, tx = hx[n % 4]
            p.dma("sp", x_, h_src[i * 128:(i + 1) * 128, cg * 512:(cg + 1) * 512], writes=[tx])
        PF = 3
        for n in range(PF):
            issue_load(n)
        load_w(wb[0][0], wb[0][1], w_out[l][:, 0:512], 16, 512)
        for n, (cg, i) in enumerate(items):
            w_, tw = wb[cg % 2]
            if i == 0 and cg + 1 < 4:
                load_w(wb[(cg + 1) % 2][0], wb[(cg + 1) % 2][1], w_out[l][:, (cg + 1) * 512:(cg + 2) * 512], 16, 512)
            if n + PF < len(items):
                issue_load(n + PF)
            x_, tx = hx[n % 4]
            n_, tn_ = hn[n % 4]
            rows = slice(i * 128, (i + 1) * 128)
            ccols = slice(cg * 512, (cg + 1) * 512)
            bk = nb()
            mm_tm(bk, A1, [tA1[i // 4]], i, w_, tw, 512, 16)
            p.op("dve", lambda e: e.tensor_tensor(n_, ps[bk][:, :], x_, ALU.add), reads=[tps[bk], tx], writes=[tn_])
            p.dma("sp", h_dst[rows, ccols], n_, reads=[tn_])
        p.barrier()

    def ffn_up_stage(l, tA1, wd):
        st = Stage(A3t, A3_BYTES, p)
        wa = [st.alloc([128, 16, 256], BF16) for _ in range(2)]
        wg = [st.alloc([128, 16, 256], BF16) for _ in range(2)]
        arow = [st.alloc([128, NT + 2], F32) for _ in range(2)]
        acc = [st.alloc([128, 512], F32) for _ in range(2)]
        ge, t_ge = st.alloc([128, 512], F32)
        ob = [st.alloc([128, NT], BF16) for _ in range(2)]
        for r in range(2):
            p.op("dve", lambda e: e.memset(arow[r][0][:, 0:2], 0.0), writes=[arow[r][1]])
        W = w_up[l]
        ffd = s_ff.rearrange("i p f t -> p i f t")
        q = 0
        def load_grp(grp):
            load_w(wa[grp % 2][0], wa[grp % 2][1], W[:, grp * 256:(grp + 1) * 256], 16, 256)
            load_w(wg[grp % 2][0], wg[grp % 2][1], W[:, DFF + grp * 256:DFF + (grp + 1) * 256], 16, 256)
        load_grp(0)
        for grp in range(22):
            wa_, twa = wa[grp % 2]
            wg_, twg = wg[grp % 2]
            if grp + 1 < 22:
                load_grp(grp + 1)
            if grp == 2:
                load_wd(l, wd, 0)
            for cc in range(2):
                f = grp * 2 + cc
                ar, tar = arow[f % 2]
                o_, to = ob[f % 2]
                for (t0, tn) in TGS:
                    ba, bb = nb(), nb()
                    tg = [tA1[t0 // 512]]
                    mm_fm(ba, wa_, twa, cc * 128, 128, A1, tg, range(16), 0, t0, tn)
                    mm_fm(bb, wg_, twg, cc * 128, 128, A1, tg, range(16), 0, t0, tn)
                    q ^= 1
                    ac_, tac = acc[q]
                    p.op("act", lambda e: e.copy(ar[:, 2 + t0:2 + t0 + tn], ps[ba][:, 0:tn]), reads=[tps[ba]], writes=[tar])

                    def cw(j):
                        k = C_CF + (l * 3 + j) * 44 + f
                        return cst[:, k:k + 1]
                    kb = C_FB + l * 44 + f
                    p.op("act", lambda e: e.activation(ac_[:, 0:tn], ar[:, t0:t0 + tn], AF.Identity, bias=cst[:, kb:kb + 1], scale=cw(0)),
                         reads=[tar, t_c], writes=[tac])
                    p.op("dve", lambda e: e.scalar_tensor_tensor(ac_[:, 0:tn], ar[:, t0 + 1:t0 + 1 + tn], cw(1), ac_[:, 0:tn], ALU.mult, ALU.add),
                         reads=[tar, tac], writes=[tac])
                    p.op("dve", lambda e: e.scalar_tensor_tensor(ac_[:, 0:tn], ar[:, t0 + 2:t0 + 2 + tn], cw(2), ac_[:, 0:tn], ALU.mult, ALU.add),
                         reads=[tar, tac], writes=[tac])
                    p.op("act", lambda e: e.activation(ge[:, 0:tn], ac_[:, 0:tn], AF.Gelu), reads=[tac], writes=[t_ge])
                    p.op("dve", lambda e: e.tensor_tensor(o_[:, t0:t0 + tn], ge[:, 0:tn], ps[bb][:, 0:tn], ALU.mult),
                         reads=[t_ge, tps[bb]], writes=[to])
                p.dma("sp", ffd[:, :, f, :], o_.rearrange("p (i t) -> p i t", t=128), reads=[to])
        p.barrier()

    def wd_views():
        return [(A2t[:, 0:44 * 512].rearrange("p (k n) -> p k n", n=512), p.tok()),
                (A1t[:, 0:44 * 512].rearrange("p (k n) -> p k n", n=512), p.tok())]

    def load_wd(l, wd, cg):
        w_, tw = wd[cg % 2]
        for hf in range(2):
            p.dma("pool", w_[:, hf * 22:(hf + 1) * 22, :],
                  w_down[l][hf * 2816:(hf + 1) * 2816, cg * 512:(cg + 1) * 512].rearrange("(kc p) n -> p kc n", p=128), writes=[tw])

    def ffn_down_stage(l, h_src, h_dst, wd):
        st = Stage(A3t, A3_BYTES, p)
        gt = [st.alloc([128, 44, 128], BF16) for _ in range(4)]
        hx = [st.alloc([128, 512], F32) for _ in range(4)]
        hn = [st.alloc([128, 512], F32) for _ in range(4)]
        items = [(cg, i) for cg in range(4) for i in range(NTI)]

        def issue_load(n):
            cg, i = items[n]
            g_, tg = gt[n % 4]
            x_, tx = hx[n % 4]
            p.dma("sp", g_, s_ff[i], writes=[tg])
            p.dma("sp", x_, h_src[i * 128:(i + 1) * 128, cg * 512:(cg + 1) * 512], writes=[tx])

        PF = 3
        for n in range(PF):
            issue_load(n)
        for n, (cg, i) in enumerate(items):
            w_, tw = wd[cg % 2]
            if i == 0 and cg + 1 < 4:
                load_wd(l, wd, cg + 1)
            if n + PF < len(items):
                issue_load(n + PF)
            g_, tg = gt[n % 4]
            x_, tx = hx[n % 4]
            n_, tn_ = hn[n % 4]
            rows = slice(i * 128, (i + 1) * 128)
            ccols = slice(cg * 512, (cg + 1) * 512)
            bk = nb()
            for kc in range(44):
                p.op("pe", lambda e: e.matmul(ps[bk][:, :], g_[:, kc, :], w_[:, kc, :], start=(kc == 0), stop=(kc == 43)),
                     reads=[tg, tw], writes=[tps[bk]], inc=(kc == 43))
            p.op("dve", lambda e: e.tensor_tensor(n_, ps[bk][:, :], x_, ALU.add), reads=[tps[bk], tx], writes=[tn_])
            p.dma("sp", h_dst[rows, ccols], n_, reads=[tn_])
        p.barrier()

    def final_stage(h_src):
        st = Stage(A3t, A3_BYTES, p)
        gB, t_gB = st.alloc([128, D], F32)
        p.dma("sp", gB, nrm[4, :].partition_broadcast(128), writes=[t_gB])
        xt = [st.alloc([128, D], F32) for _ in range(2)]
        yo = [st.alloc([128, D], F32) for _ in range(2)]
        junk, t_junk = st.alloc([128, D], BF16)
        ss = [st.alloc([128, 1], F32) for _ in range(2)]
        def ftile(i):
            b = i % 2
            x_, tx = xt[b]
            y_, ty = yo[b]
            s_, ts = ss[b]
            p.dma("sp", x_, h_src[i * 128:(i + 1) * 128, :], writes=[tx])
            p.op("dve", lambda e: e.memset(s_, 0.0), writes=[ts])
            yield
            p.op("act", lambda e: e.activation(junk, x_, AF.Square, accum_out=s_), reads=[tx], writes=[t_junk, ts])
            yield
            rstd_from_ss(s_, ts, D)
            yield
            p.op("dve", lambda e: e.scalar_tensor_tensor(y_, x_, s_, gB, ALU.mult, ALU.mult), reads=[tx, ts, t_gB], writes=[ty])
            yield
            for hf in range(2):
                p.dma("pool", y[(i - 1) * 128:i * 128, hf * 1024:(hf + 1) * 1024], y_[:, hf * 1024:(hf + 1) * 1024], reads=[ty])
        for i0 in range(1, NTI, 2):
            run_interleaved([ftile(i) for i in range(i0, min(NTI, i0 + 2))])

    def body():
        h_cur = h0
        for l in range(n_layers):
            tA1 = p.toks(5)
            tA2a, tA2m, tA2d = p.tok(), p.tok(), p.tok()
            with nc.named_scope('norm_stage'):
                norm_stage(h_cur, 2 * l, A1, tA1)
            tap("u%d" % l, A1t[:, :], [128, A_EL], tA1, BF16)
            if stop == "norm%d" % l:
                return
            with nc.named_scope('zproj_stage'):
                zproj_stage(l, tA1, tA2a)
            tap("prea%d" % l, A2t[:, 0:4 * NT], [128, 4 * NT], [tA2a], BF16)
            tap("sqT%d" % l, s_qT, [1024, NT], [], BF16)
            tap("sk%d" % l, s_k, [NT, 1024], [], BF16)
            tap("sgi%d" % l, s_gi, [4, NT], [])
            tap("sg%d" % l, s_g, [6144, NT], [], BF16)
            if stop == "zproj%d" % l:
                return
            with nc.named_scope('mlstm_stage'):
                mlstm_stage(l, tA2m)
            tap("hm%d" % l, A2t[:, 4 * NT:12 * NT], [128, 8 * NT], [tA2m], BF16)
            if stop == "mlstm%d" % l:
                return
            with nc.named_scope('attn_stage'):
                attn_stage(l, tA2d)
            tap("hd%d" % l, A2t[:, 12 * NT:16 * NT], [128, 4 * NT], [tA2d], BF16)
            if stop == "attn%d" % l:
                return
            tA1 = p.toks(5)
            with nc.named_scope('merge_stage'):
                merge_stage(l, [tA2a, tA2m, tA2d], tA1)
            tap("merged%d" % l, A1t[:, :], [128, A_EL], tA1, BF16)
            with nc.named_scope('wout_stage'):
                wout_stage(l, h_cur, hA, tA1)
            tap("hmix%d" % l, hA, [NT, D], [])
            if stop == "wout%d" % l:
                return
            tA1 = p.toks(5)
            with nc.named_scope('norm_stage'):
                norm_stage(hA, 2 * l + 1, A1, tA1)
            with nc.named_scope('ffn_up_stage'):
                wd = wd_views()
                ffn_up_stage(l, tA1, wd)
            with nc.named_scope('ffn_down_stage'):
                ffn_down_stage(l, hA, hB, wd)
            tap("hffn%d" % l, hB, [NT, D], [])
            if stop == "ffn%d" % l:
                return
            h_cur = hB
        final_stage(h_cur)
    body()
    p.finish()
    return nc, p


def _host_consts(inputs):
    cst = np.zeros((128, NCST), np.float32)
    cst[:, C_ID:C_ID + 128] = np.eye(128, dtype=np.float32)
    s = np.arange(128)[:, None]
    t = np.arange(128)[None, :]
    cst[:, C_TRI:C_TRI + 128] = (s <= t).astype(np.float32)
    cst[:, C_MV:C_MV + 128] = np.where(t < 112, NEG, 0.0)
    cst[:, C_MD:C_MD + 128] = np.where((s < 64) & (t >= 64), NEG, 0.0)
    ca = np.asarray(inputs["conv_a"], np.float32)
    cst[:, C_CA:C_CA + 24] = ca.reshape(2, 3, 4, 128).transpose(3, 0, 1, 2).reshape(128, 24)
    cf = np.asarray(inputs["conv_ffn"], np.float32)
    cst[:, C_CF:C_CF + 264] = cf.reshape(2, 3, 44, 128).transpose(3, 0, 1, 2).reshape(128, 264)
    fb = np.asarray(inputs["conv_ffn_b"], np.float32)
    cst[:, C_FB:C_FB + 88] = fb.reshape(2, 44, 128).transpose(2, 0, 1).reshape(128, 88)
    c68 = np.zeros((68, 260), np.float32)
    bif = np.asarray(inputs["b_if"], np.float32)
    for l in range(2):
        for w in range(2):
            c68[:, 2 * l + w] = np.repeat(bif[l, w], 17)
    c68[:, 132:260] = 1.0
    for h in range(4):
        c68[17 * h, 4:4 + 112] = -1.0e4
        c68[17 * h, 132:132 + 112] = 0.0
    return cst, c68


_CACHE = {}


def kernel(**inputs):
    x = np.asarray(inputs["x"], np.float32)
    meta = np.asarray(inputs["meta"], np.float32)
    B = x.shape[0]
    cst, c68 = _host_consts(inputs)
    nrm = np.ascontiguousarray(np.stack([inputs["norm_mix"][0], inputs["norm_ffn"][0], inputs["norm_mix"][1],
                                         inputs["norm_ffn"][1], inputs["norm_f"]]).astype(np.float32))
    w_br = np.ascontiguousarray(np.concatenate([inputs["w_br_a"], inputs["w_br_m"], inputs["w_br_d"]], axis=1).astype(np.float32))
    shared = {
        "w_in": np.ascontiguousarray(inputs["w_in"], np.float32),
        "w_br": w_br,
        "w_out": np.ascontiguousarray(inputs["w_out"], np.float32),
        "w_up": np.ascontiguousarray(inputs["w_up"], np.float32),
        "w_down": np.ascontiguousarray(inputs["w_down"], np.float32),
        "nrm": nrm, "cst": cst, "c68": c68,
        "mlg": np.ascontiguousarray(inputs["ml_norm"], np.float32),
        "dan": np.ascontiguousarray(inputs["da_norm"], np.float32),
        "dal": np.ascontiguousarray(np.asarray(inputs["da_lambda"], np.float32).reshape(2, 256)),
    }
    in_maps = []
    for b in range(B):
        h0 = np.zeros((NT, D), np.float32)
        h0[112:128] = meta
        h0[128:] = x[b]
        m = dict(shared)
        m["h0"] = h0
        in_maps.append(m)
    nc = build_program()[0]
    res = run_bass_kernel_spmd(nc, in_maps, core_ids=list(range(B)))
    return np.stack([np.asarray(r["y"], np.float32) for r in res.results], axis=0)
```
